# Optimizing a Trainium2 kernel written in Bass

```python
import math
import jax, jax.numpy as jnp
from jax import lax
import numpy as np

D_MODEL = 2048
BATCH = 1
SEQ = 16384
DEPTH = 4

N_MIXERS = 4
N_HEADS = 16
HEAD_DIM = D_MODEL // N_HEADS
N_ATT_HEADS = 8
ATT_HEAD_DIM = D_MODEL // N_ATT_HEADS
D_FF = 4 * D_MODEL
CHUNK = 16
Q_BLOCK = 128
CONV_WIDTH = 3
LN_EPS = 1e-5
RMS_EPS = 1e-6
ALPHA = (2.0 * DEPTH) ** 0.25
BETA = (8.0 * DEPTH) ** -0.25
N_LAYERS_A = (DEPTH + 3) // 4
N_LAYERS_B = (DEPTH + 2) // 4
N_LAYERS_C = (DEPTH + 1) // 4
N_LAYERS_D = DEPTH // 4

kernel_name = "hybrid_hgrn2_stickbreak_fox_shortconv"


def _heads(t, n_heads):
    b, s, _ = t.shape
    return t.reshape(b, s, n_heads, -1).transpose(0, 2, 1, 3)


def _merge(t):
    b, h, s, d = t.shape
    return t.transpose(0, 2, 1, 3).reshape(b, s, h * d)


def layer_norm(x, g, b):
    xf = x.astype(jnp.float32)
    mu = jnp.mean(xf, axis=-1, keepdims=True)
    var = jnp.mean(jnp.square(xf - mu), axis=-1, keepdims=True)
    return ((xf - mu) * lax.rsqrt(var + LN_EPS) * g + b).astype(x.dtype)


def _block_mask(i, strict):
    q_pos = i * Q_BLOCK + jnp.arange(Q_BLOCK)
    k_pos = jnp.arange((i + 1) * Q_BLOCK).reshape(i + 1, Q_BLOCK)
    if strict:
        return k_pos[None] < q_pos[:, None, None]
    return k_pos[None] <= q_pos[:, None, None]


def gated_linear_recurrence(q, k, v, log_f):
    bsz, h, s, dk = q.shape
    dv = v.shape[-1]
    n = s // CHUNK

    def to_chunks(t):
        return jnp.moveaxis(t.astype(jnp.float32).reshape(bsz, h, n, CHUNK, t.shape[-1]), 2, 0)

    qc, kc, vc, gc = to_chunks(q), to_chunks(k), to_chunks(v), to_chunks(log_f)
    bc = jnp.cumsum(gc, axis=-2)
    b_last = bc[..., -1:, :]
    q_dec = qc * jnp.exp(bc)
    k_inv = kc * jnp.exp(-bc)
    k_dec = kc * jnp.exp(b_last - bc)
    causal = jnp.tril(jnp.ones((CHUNK, CHUNK), dtype=bool))
    scores = jnp.where(causal, jnp.einsum('nbhtd,nbhsd->nbhts', q_dec, k_inv), 0.0)
    intra = jnp.einsum('nbhts,nbhsv->nbhtv', scores, vc)

    def step(state, inp):
        qd, kd, vb, dl = inp
        inter = jnp.einsum('bhtd,bhdv->bhtv', qd, state)
        new_state = dl[..., 0, :, None] * state + jnp.einsum('bhsd,bhsv->bhdv', kd, vb)
        return new_state, inter

    state0 = jnp.zeros((bsz, h, dk, dv), jnp.float32)
    _, inter = lax.scan(step, state0, (q_dec, k_dec, vc, jnp.exp(b_last)))
    return jnp.moveaxis(intra + inter, 0, 2).reshape(bsz, h, s, dv)


def hgrn2_mixer(x, w_in, norm_g, w_out, lb):
    q, f, i, g = jnp.split(x @ w_in, 4, axis=-1)
    q = jax.nn.silu(q.astype(jnp.float32)) * HEAD_DIM ** -0.5
    f = f.astype(jnp.float32)
    log_f = jnp.logaddexp(jnp.log(lb), jnp.log1p(-lb) + jax.nn.log_sigmoid(f))
    k = (1.0 - lb) * jax.nn.sigmoid(-f)
    o = gated_linear_recurrence(_heads(q, N_HEADS), _heads(k, N_HEADS),
                                _heads(i, N_HEADS), _heads(log_f, N_HEADS))
    o = o * lax.rsqrt(jnp.mean(jnp.square(o), axis=-1, keepdims=True) + RMS_EPS) * norm_g
    o = _merge(o) * jax.nn.silu(g.astype(jnp.float32))
    return o.astype(x.dtype) @ w_out


def stick_breaking_mixer(x, w_in, w_out):
    q, k, v = [_heads(t, N_ATT_HEADS) for t in jnp.split(x @ w_in, 3, axis=-1)]
    bsz, h, s, d = q.shape
    nb = s // Q_BLOCK
    q = q * d ** -0.5
    k_blk = k.reshape(bsz, h, nb, Q_BLOCK, d)
    v_blk = v.reshape(bsz, h, nb, Q_BLOCK, d)
    incl_rev = jnp.tril(jnp.ones((Q_BLOCK, Q_BLOCK), jnp.float32))
    outs = []
    for i in range(nb):
        qb = q[:, :, i * Q_BLOCK:(i + 1) * Q_BLOCK]
        kp, vp = k_blk[:, :, :i + 1], v_blk[:, :, :i + 1]
        z = jnp.einsum('bhtd,bhnsd->bhtns', qb, kp).astype(jnp.float32)
        z = jnp.where(_block_mask(i, strict=True), z, -jnp.inf)
        sp = jax.nn.relu(z) + jnp.log1p(jnp.exp(-jnp.abs(z)))
        within = jnp.einsum('bhtnj,js->bhtns', sp, incl_rev)
        later_blocks = jnp.tril(jnp.ones((i + 1, i + 1), jnp.float32), -1)
        offs = jnp.einsum('bhtm,mn->bhtn', within[..., 0], later_blocks)
        w = jnp.exp(z - (within + offs[..., None]))
        outs.append(jnp.einsum('bhtns,bhnsd->bhtd', w.astype(vp.dtype), vp))
    out = jnp.concatenate(outs, axis=2)
    return _merge(out) @ w_out


def forgetting_mixer(x, w_in, b_f, w_out):
    proj = x @ w_in
    q, k, v = [_heads(t, N_ATT_HEADS) for t in jnp.split(proj[..., :3 * D_MODEL], 3, axis=-1)]
    log_f = jax.nn.log_sigmoid((proj[..., 3 * D_MODEL:] + b_f).astype(jnp.float32))
    cum = jnp.cumsum(log_f, axis=1).transpose(0, 2, 1)
    bsz, h, s, d = q.shape
    nb = s // Q_BLOCK
    q = q * d ** -0.5
    k_blk = k.reshape(bsz, h, nb, Q_BLOCK, d)
    v_blk = v.reshape(bsz, h, nb, Q_BLOCK, d)
    cum_blk = cum.reshape(bsz, h, nb, Q_BLOCK)
    outs = []
    for i in range(nb):
        qb = q[:, :, i * Q_BLOCK:(i + 1) * Q_BLOCK]
        kp, vp = k_blk[:, :, :i + 1], v_blk[:, :, :i + 1]
        cq = cum_blk[:, :, i]
        scores = jnp.einsum('bhtd,bhnsd->bhtns', qb, kp).astype(jnp.float32)
        bias = cq[:, :, :, None, None] - cum_blk[:, :, None, :i + 1, :]
        logits = jnp.where(_block_mask(i, strict=False), scores + bias, -jnp.inf)
        e = jnp.exp(logits - jnp.max(logits, axis=(3, 4), keepdims=True))
        denom = jnp.sum(e, axis=(3, 4))
        o = jnp.einsum('bhtns,bhnsd->bhtd', e.astype(vp.dtype), vp)
        outs.append((o / denom[..., None]).astype(vp.dtype))
    out = jnp.concatenate(outs, axis=2)
    return _merge(out) @ w_out


def short_conv_mixer(x, w_in, conv_w, w_out):
    b_gate, c_gate, hid = jnp.split(x @ w_in, 3, axis=-1)
    u = c_gate * hid
    y = lax.conv_general_dilated(
        u, conv_w.astype(u.dtype)[:, None, :], window_strides=(1,),
        padding=[(CONV_WIDTH - 1, 0)], dimension_numbers=('NWC', 'WIO', 'NWC'),
        feature_group_count=D_MODEL)
    return (b_gate * y) @ w_out


def squared_relu_mlp(x, w1, w2):
    return jnp.square(jax.nn.relu(x @ w1)) @ w2


def setup_inputs(seed: int = 0) -> dict:
    key = jax.random.key(seed)
    ks = jax.random.split(key, 24)
    d = D_MODEL

    def nrm(k, shape, scale):
        return jax.random.normal(k, shape, jnp.float32) * scale

    return {
        "x": nrm(ks[0], (BATCH, SEQ, d), 1.0),
        "w_mix_a": nrm(ks[1], (N_LAYERS_A, d, 4 * d), d ** -0.5),
        "norm_g_a": 1.0 + nrm(ks[2], (N_LAYERS_A, HEAD_DIM), 0.02),
        "lb_logits": nrm(ks[3], (DEPTH + 1, d), 0.5),
        "w_out_a": nrm(ks[4], (N_LAYERS_A, d, d), BETA * d ** -0.5),
        "w_mix_b": nrm(ks[5], (N_LAYERS_B, d, 3 * d), d ** -0.5),
        "w_out_b": nrm(ks[6], (N_LAYERS_B, d, d), BETA * d ** -0.5),
        "w_mix_c": nrm(ks[7], (N_LAYERS_C, d, 3 * d + N_ATT_HEADS), d ** -0.5),
        "b_f_c": nrm(ks[8], (N_LAYERS_C, N_ATT_HEADS), 0.1),
        "w_out_c": nrm(ks[9], (N_LAYERS_C, d, d), BETA * d ** -0.5),
        "w_mix_d": nrm(ks[10], (N_LAYERS_D, d, 3 * d), d ** -0.5),
        "conv_w_d": nrm(ks[11], (N_LAYERS_D, CONV_WIDTH, d), CONV_WIDTH ** -0.5),
        "w_out_d": nrm(ks[12], (N_LAYERS_D, d, d), BETA * d ** -0.5),
        "ln_mix_g": 1.0 + nrm(ks[13], (DEPTH, d), 0.02),
        "ln_mix_b": nrm(ks[14], (DEPTH, d), 0.02),
        "w_ff1": nrm(ks[15], (DEPTH, d, D_FF), d ** -0.5),
        "w_ff2": nrm(ks[16], (DEPTH, D_FF, d), BETA * D_FF ** -0.5),
        "ln_ff_g": 1.0 + nrm(ks[17], (DEPTH, d), 0.02),
        "ln_ff_b": nrm(ks[18], (DEPTH, d), 0.02),
    }


def reference(x, w_mix_a, norm_g_a, lb_logits, w_out_a, w_mix_b, w_out_b,
              w_mix_c, b_f_c, w_out_c, w_mix_d, conv_w_d, w_out_d,
              ln_mix_g, ln_mix_b, w_ff1, w_ff2, ln_ff_g, ln_ff_b):
    lb_table = jnp.cumsum(jax.nn.softmax(lb_logits.astype(jnp.float32), axis=0), axis=0)
    h = x
    for i in range(DEPTH):
        m, j = i % N_MIXERS, i // N_MIXERS
        if m == 0:
            y = hgrn2_mixer(h, w_mix_a[j], norm_g_a[j], w_out_a[j], lb_table[i])
        elif m == 1:
            y = stick_breaking_mixer(h, w_mix_b[j], w_out_b[j])
        elif m == 2:
            y = forgetting_mixer(h, w_mix_c[j], b_f_c[j], w_out_c[j])
        else:
            y = short_conv_mixer(h, w_mix_d[j], conv_w_d[j], w_out_d[j])
        h = layer_norm(ALPHA * h + y, ln_mix_g[i], ln_mix_b[i])
        h = layer_norm(ALPHA * h + squared_relu_mlp(h, w_ff1[i], w_ff2[i]), ln_ff_g[i], ln_ff_b[i])
    return h
```

```python
import contextlib
import numpy as np
import concourse.bass as bass
import concourse.mybir as mybir
from concourse.bass_utils import run_bass_kernel_spmd

F32 = mybir.dt.float32
BF16 = mybir.dt.bfloat16
AF = mybir.ActivationFunctionType
ALU = mybir.AluOpType
AX = mybir.AxisListType

SEM_CHUNK = 20000
SAME_ENGINE_SYNC = True


class Buf:
    __slots__ = ("name", "lastw", "readers", "dsems", "dcount")

    def __init__(self, name):
        self.name = name
        self.lastw = None
        self.readers = []
        self.dsems = []
        self.dcount = 0


class Prog:
    ENGS = ("pe", "act", "dve", "pool", "sp")

    def __init__(self, nc):
        self.nc = nc
        self.ops = []
        self.dry = False
        self.stack = contextlib.ExitStack()
        self.nbuf = 0

    def sb(self, name, shape, dt):
        return self.stack.enter_context(self.nc.sbuf_tensor("sb_" + name, list(shape), dt))

    def ps(self, name, shape, dt=F32):
        return self.stack.enter_context(self.nc.psum_tensor("pm_" + name, list(shape), dt))

    def buf(self, name=None):
        self.nbuf += 1
        return Buf(name or f"b{self.nbuf}")

    def bufs(self, n, name="b"):
        return [self.buf(f"{name}{i}") for i in range(n)]

    def op(self, eng, fn, reads=(), writes=(), dma=False):
        if self.dry:
            return
        self.ops.append(dict(eng=eng, fn=fn, reads=list(reads), writes=list(writes), dma=dma))

    def mm(self, out, lhsT, rhs, start, stop, reads, writes):
        self.op("pe", lambda e: e.matmul(out, lhsT, rhs, start=start, stop=stop), reads, writes)

    def act(self, out, in_, func, reads, writes, **kw):
        self.op("act", lambda e: e.activation(out=out, in_=in_, func=func, **kw), reads, writes)

    def dma(self, eng, out, in_, reads, writes, **kw):
        self.op(eng, lambda e: e.dma_start(out=out, in_=in_, **kw), reads, writes, dma=True)

    def wait_all(self, eng, bufs):
        self.op(eng, None, reads=list(bufs), writes=[])

    def finish(self):
        nc = self.nc
        ops = self.ops
        for i, o in enumerate(ops):
            deps = set()
            for b in o["reads"]:
                if b.lastw is not None:
                    deps.add(b.lastw)
            for b in o["writes"]:
                if b.lastw is not None:
                    deps.add(b.lastw)
                deps.update(b.readers)
            for b in o["reads"]:
                b.readers.append(i)
            for b in o["writes"]:
                b.lastw = i
                b.readers = []
            deps.discard(i)
            fd = []
            for j in deps:
                oj = ops[j]
                if not oj["dma"]:
                    if oj["fn"] is None:
                        continue
                    if oj["eng"] == o["eng"]:
                        if o["eng"] == "pe" or not SAME_ENGINE_SYNC:
                            continue
                    oj["signal"] = True
                fd.append(j)
            o["deps"] = fd
            if o["dma"]:
                b = o["writes"][0]
                if b.dcount + 16 > SEM_CHUNK or not b.dsems:
                    b.dsems.append(nc.alloc_semaphore(name=f"d_{b.name}_{len(b.dsems)}"))
                    b.dcount = 0
                b.dcount += 16
                o["dsem"] = (b.dsems[-1], b.dcount)
        cnt = {e: 0 for e in self.ENGS}
        esems = {e: [] for e in self.ENGS}
        for o in ops:
            if o.get("signal") and not o["dma"]:
                e = o["eng"]
                c = cnt[e]
                if c % SEM_CHUNK == 0:
                    esems[e].append(nc.alloc_semaphore(name=f"s_{e}_{len(esems[e])}"))
                cnt[e] += 1
                o["sig"] = (esems[e][-1], c % SEM_CHUNK + 1)
        per_eng = {e: [o for o in ops if o["eng"] == e] for e in self.ENGS}
        nwaits = {e: 0 for e in self.ENGS}

        def emit(engname, eng):
            waited = {}
            for o in per_eng[engname]:
                need = {}
                for j in o["deps"]:
                    oj = ops[j]
                    sem, val = oj["dsem"] if oj["dma"] else oj["sig"]
                    k = id(sem)
                    if k not in need or need[k][1] < val:
                        need[k] = (sem, val)
                for k, (sem, val) in need.items():
                    if waited.get(k, 0) >= val:
                        continue
                    eng.wait_ge(sem, val)
                    waited[k] = val
                    nwaits[engname] += 1
                if o["fn"] is None:
                    continue
                ins = o["fn"](eng)
                if o["dma"]:
                    ins.then_inc(o["dsem"][0], 16)
                elif o.get("signal"):
                    ins.then_inc(o["sig"][0], 1)

        with nc.Block() as block:
            @block.tensor
            def _(e):
                emit("pe", e)

            @block.scalar
            def _(e):
                emit("act", e)

            @block.vector
            def _(e):
                emit("dve", e)

            @block.gpsimd
            def _(e):
                emit("pool", e)

            @block.sync
            def _(e):
                emit("sp", e)
        self.stats = dict(n_ops={e: len(per_eng[e]) for e in self.ENGS}, n_waits=nwaits)
        self.stack.close()
D = 2048
DFF = 8192
NCH = 16
TT = 512
ALPHA = (2.0 * 4) ** 0.25
LN_EPS = 1e-5
NWSLOT = 4


class WStream:
    def __init__(self, p):
        self.p = p
        self.plan = []
        self.pos = 0
        self.issued = 0
        self.slots = None
        self.sbufs = None

    def alloc(self):
        p = self.p
        self.slots = p.sb("wslots", [128, NWSLOT, 16, 256], BF16)
        self.sbufs = p.bufs(NWSLOT, "wslot")

    def _issue(self, i):
        W, k0, c0, bc = self.plan[i]
        s = i % NWSLOT
        src = W.rearrange("(k p) n -> p k n", p=128)[:, k0:k0 + 16, c0:c0 + bc]
        self.p.dma("pool", self.slots[:, s, :, 0:bc], src, reads=[], writes=[self.sbufs[s]])

    def next(self, W, k0, c0, bc=256):
        p = self.p
        if p.dry:
            self.plan.append((W, k0, c0, bc))
            return None, None
        i = self.pos
        self.pos += 1
        while self.issued < len(self.plan) and self.issued <= i + (NWSLOT - 1):
            self._issue(self.issued)
            self.issued += 1
        s = i % NWSLOT
        return self.slots[:, s], self.sbufs[s]


def build_row(nc, pro, epi, ntok=2048):
    p = Prog(nc)
    NT = ntok // TT

    def din(name, shape, dt=F32):
        return nc.dram_tensor(name, list(shape), dt, kind="ExternalInput").ap()

    def dout(name, shape, dt=F32):
        return nc.dram_tensor(name, list(shape), dt, kind="ExternalOutput").ap()

    hin = din("hin", [D, ntok])
    if pro == "A2":
        on_d = din("on", [D, ntok])
        gs_d = din("gs", [D, ntok])
    if pro == "attn":
        oT_d = din("oT", [D, ntok], BF16)
    if pro == "conv":
        uT_d = din("uT", [D, ntok + 2])
        bT_d = din("bT", [D, ntok])
        cw_d = din("cw", [128, 3 * NCH])
    if pro != "x":
        wout_d = din("w_out", [D, D])
        w1_d = din("w1", [D, DFF])
        w2_d = din("w2", [DFF, D])
        lnp_d = din("lnp", [128, 4 * NCH])
    ones_d = din("ones", [128, 128])
    if epi == "A":
        wmix_d = din("w_mix", [D, 4 * D])
        qT_o = dout("qT", [D, ntok]); fT_o = dout("fT", [D, ntok]); gs_o = dout("gs_o", [D, ntok])
        itok_o = dout("itok", [ntok, D], BF16)
    elif epi in ("B", "C"):
        ncol = 3 * D + (8 if epi == "C" else 0)
        wmix_d = din("w_mix", [D, ncol])
        qT_o = dout("qT", [D, ntok], BF16); kT_o = dout("kT", [D, ntok], BF16)
        v_o = dout("v", [ntok, D], BF16)
        if epi == "C":
            fl_o = dout("fl", [8, ntok])
    elif epi == "D":
        wmix_d = din("w_mix", [D, 3 * D])
        uT_o = dout("uT_o", [D, ntok]); bT_o = dout("bT_o", [D, ntok])
    if pro != "x":
        hout = dout("hout", [D, ntok])

    S = p.sb("S", [128, NCH, TT], F32); S_b = p.bufs(NCH, "S")
    hb = p.sb("hb", [128, NCH, TT], BF16); hb_b = p.bufs(NCH, "hb")
    u = p.sb("u", [128, 64, TT], BF16); u_b = p.bufs(64, "u")
    NSCR = 8
    scr = p.sb("scr", [128, NSCR, TT], F32); scr_b = p.bufs(NSCR, "scr")
    st = p.sb("st", [128, 4, TT], F32); st_b = p.bufs(4, "st")
    ones = p.sb("ones", [128, 128], BF16); ones_b = p.buf("ones")
    cst = p.sb("cst", [128, 8 * NCH], F32); cst_b = p.buf("cst")
    NPS = 6
    pst = [p.ps(f"ps{i}", [128, TT]) for i in range(NPS)]; ps_b = p.bufs(NPS, "ps")
    pstat = [p.ps(f"pstat{i}", [128, TT]) for i in range(2)]; pstat_b = p.bufs(2, "pstat")
    ws = WStream(p)
    ws.alloc()
    out_bufs = {}

    def obuf(name):
        if name not in out_bufs:
            out_bufs[name] = p.buf("o_" + name)
        return out_bufs[name]

    state = dict(scr=0, ps=0, ev=0)
    pend = []

    def flush():
        for f in pend:
            f()
        pend.clear()

    def nscr():
        i = state["scr"] % NSCR
        state["scr"] += 1
        return i

    def nps():
        i = state["ps"] % NPS
        state["ps"] += 1
        return i

    def scr_bf(i):
        return scr[:, i].bitcast(BF16)[:, 0:TT]

    p.dma("pool", ones[:], ones_d[:, :], reads=[], writes=[ones_b])
    if pro != "x":
        p.dma("sp", cst[:, 0:4 * NCH], lnp_d[:, :], reads=[], writes=[cst_b])
    if pro == "conv":
        p.dma("sp", cst[:, 4 * NCH:7 * NCH], cw_d[:, :], reads=[], writes=[cst_b])

    def lncol(which, o):
        return cst[:, which * NCH + o: which * NCH + o + 1]

    def mm_fm(W, col0, nout, nk, rhs, rhs_bufs, epilogue):
        nkb = nk // 16
        for cb in range(nout // 2):
            banks = [nps(), nps()]
            for kb in range(nkb):
                slot, sbuf = ws.next(W, kb * 16, col0 + cb * 256)
                if p.dry:
                    continue
                for oc in range(2):
                    for k in range(16):
                        kk = kb * 16 + k
                        p.mm(pst[banks[oc]][:], slot[:, k, oc * 128:(oc + 1) * 128], rhs(kk),
                             start=(kk == 0), stop=(kk == nk - 1),
                             reads=[sbuf, rhs_bufs[kk]], writes=[ps_b[banks[oc]]])
            if p.dry:
                continue
            flush()
            for oc in range(2):
                epilogue(cb * 2 + oc, pst[banks[oc]], ps_b[banks[oc]])
        flush()

    def mm_tm(W, col0, ncols, epilogue):
        for cb in range(ncols // 256):
            slot, sbuf = ws.next(W, 0, col0 + cb * 256)
            if p.dry:
                continue
            for tb in range(TT // 128):
                b = nps()
                for k in range(16):
                    p.mm(pst[b][:, 0:256], hb[:, k, tb * 128:(tb + 1) * 128], slot[:, k, :],
                         start=(k == 0), stop=(k == 15), reads=[sbuf, hb_b[k]], writes=[ps_b[b]])
                epilogue(cb, tb, pst[b], ps_b[b])

    def ln_stats_chunk(o, first, last):
        i1 = nscr(); i2 = nscr()
        p.act(scr_bf(i1), S[:, o], AF.Copy, reads=[S_b[o]], writes=[scr_b[i1]])
        p.act(scr_bf(i2), S[:, o], AF.Square, reads=[S_b[o]], writes=[scr_b[i2]])
        pend.append(lambda: p.mm(pstat[0][:], ones[:], scr_bf(i1), start=first, stop=last,
                                 reads=[ones_b, scr_b[i1]], writes=[pstat_b[0]]))
        pend.append(lambda: p.mm(pstat[1][:], ones[:], scr_bf(i2), start=first, stop=last,
                                 reads=[ones_b, scr_b[i2]], writes=[pstat_b[1]]))

    def ln_apply(gi, bi):
        V = "dve"
        p.op(V, lambda e: e.tensor_scalar(st[:, 0], pstat[0][:], 1.0 / D, None, ALU.mult),
             reads=[pstat_b[0]], writes=[st_b[0]])
        p.op(V, lambda e: e.tensor_tensor(st[:, 1], st[:, 0], st[:, 0], ALU.mult),
             reads=[st_b[0]], writes=[st_b[1]])
        p.op(V, lambda e: e.scalar_tensor_tensor(st[:, 2], pstat[1][:], 1.0 / D, st[:, 1], ALU.mult, ALU.subtract),
             reads=[pstat_b[1], st_b[1]], writes=[st_b[2]])
        p.op(V, lambda e: e.tensor_scalar(st[:, 2], st[:, 2], LN_EPS, None, ALU.add),
             reads=[st_b[2]], writes=[st_b[2]])
        p.act(st[:, 2], st[:, 2], AF.Sqrt, reads=[st_b[2]], writes=[st_b[2]])
        p.op(V, lambda e: e.reciprocal(st[:, 1], st[:, 2]), reads=[st_b[2]], writes=[st_b[1]])
        p.op(V, lambda e: e.scalar_tensor_tensor(st[:, 3], st[:, 0], -1.0, st[:, 1], ALU.mult, ALU.mult),
             reads=[st_b[0], st_b[1]], writes=[st_b[3]])
        for o in range(NCH):
            i1 = nscr()
            gcol = lncol(gi, o); bcol = lncol(bi, o)
            p.op(V, lambda e, o=o, i1=i1: e.tensor_tensor(scr[:, i1], S[:, o], st[:, 1], ALU.mult),
                 reads=[S_b[o], st_b[1]], writes=[scr_b[i1]])
            i2 = nscr()
            p.op(V, lambda e, i1=i1, i2=i2: e.tensor_tensor(scr[:, i2], scr[:, i1], st[:, 3], ALU.add),
                 reads=[scr_b[i1], st_b[3]], writes=[scr_b[i2]])
            p.act(S[:, o], scr[:, i2], AF.Identity, reads=[scr_b[i2], cst_b], writes=[S_b[o]],
                  scale=gcol, bias=bcol)
            p.act(hb[:, o], scr[:, i2], AF.Identity, reads=[scr_b[i2], cst_b], writes=[hb_b[o]],
                  scale=gcol, bias=bcol)

    def resid_epilogue(o, ps, psb):
        p.op("dve", lambda e: e.scalar_tensor_tensor(S[:, o], S[:, o], ALPHA, ps[:], ALU.mult, ALU.add),
             reads=[S_b[o], psb], writes=[S_b[o]])
        ln_stats_chunk(o, o == 0, o == NCH - 1)

    def relu2_epilogue(o, ps, psb):
        i1 = nscr()
        p.act(scr[:, i1], ps[:], AF.Relu, reads=[psb], writes=[scr_b[i1]])
        p.op("dve", lambda e: e.tensor_tensor(u[:, o], scr[:, i1], scr[:, i1], ALU.mult),
             reads=[scr_b[i1]], writes=[u_b[o]])

    def evac(dst, src, reads, writes, func=None, scale=None):
        state["ev"] += 1
        if func is not None or state["ev"] % 2 == 0:
            kw = {} if scale is None else dict(scale=scale)
            p.act(dst, src, func or AF.Copy, reads=reads, writes=writes, **kw)
        else:
            if scale is None:
                p.op("dve", lambda e: e.tensor_copy(dst, src), reads=reads, writes=writes)
            else:
                p.op("dve", lambda e: e.tensor_scalar(dst, src, scale, None, ALU.mult), reads=reads, writes=writes)

    def store_fm(dst_d, o, t, src_ap, src_buf, name):
        p.dma("sp", dst_d[o * 128:(o + 1) * 128, t * TT:(t + 1) * TT], src_ap, reads=[src_buf], writes=[obuf(name)])

    def record():
        ws.pos = 0
        for t in range(NT):
            tsl = slice(t * TT, (t + 1) * TT)
            for o in range(NCH):
                p.dma("sp", S[:, o], hin[o * 128:(o + 1) * 128, tsl], reads=[], writes=[S_b[o]])
            if pro == "x":
                for o in range(NCH):
                    evac(hb[:, o], S[:, o], [S_b[o]], [hb_b[o]])
            aT = u
            if pro == "attn":
                for o in range(NCH):
                    p.dma("sp", aT[:, o], oT_d[o * 128:(o + 1) * 128, tsl], reads=[], writes=[u_b[o]])
            if pro == "A2":
                for o in range(NCH):
                    i1 = nscr(); i2 = nscr()
                    p.dma("sp", scr[:, i1], on_d[o * 128:(o + 1) * 128, tsl], reads=[], writes=[scr_b[i1]])
                    p.dma("sp", scr[:, i2], gs_d[o * 128:(o + 1) * 128, tsl], reads=[], writes=[scr_b[i2]])
                    p.op("dve", lambda e, o=o, i1=i1, i2=i2: e.tensor_tensor(aT[:, o], scr[:, i1], scr[:, i2], ALU.mult),
                         reads=[scr_b[i1], scr_b[i2]], writes=[u_b[o]])
            if pro == "conv":
                for o in range(NCH):
                    i1 = nscr(); i2 = nscr(); i3 = nscr()
                    rows = slice(o * 128, (o + 1) * 128)
                    p.dma("sp", scr[:, i1], uT_d[rows, t * TT + 2:t * TT + 2 + TT], reads=[], writes=[scr_b[i1]])
                    p.dma("sp", scr[:, i2], uT_d[rows, t * TT + 1:t * TT + 1 + TT], reads=[], writes=[scr_b[i2]])
                    p.dma("sp", scr[:, i3], uT_d[rows, t * TT:t * TT + TT], reads=[], writes=[scr_b[i3]])
                    w2c = cst[:, 4 * NCH + 2 * NCH + o:4 * NCH + 2 * NCH + o + 1]
                    w1c = cst[:, 4 * NCH + 1 * NCH + o:4 * NCH + 1 * NCH + o + 1]
                    w0c = cst[:, 4 * NCH + 0 * NCH + o:4 * NCH + 0 * NCH + o + 1]
                    i4 = nscr()
                    p.op("dve", lambda e, i1=i1, i4=i4, w2c=w2c: e.tensor_scalar(scr[:, i4], scr[:, i1], w2c, None, ALU.mult),
                         reads=[scr_b[i1], cst_b], writes=[scr_b[i4]])
                    i5 = nscr()
                    p.op("dve", lambda e, i2=i2, i4=i4, i5=i5, w1c=w1c: e.scalar_tensor_tensor(scr[:, i5], scr[:, i2], w1c, scr[:, i4], ALU.mult, ALU.add),
                         reads=[scr_b[i2], scr_b[i4], cst_b], writes=[scr_b[i5]])
                    i6 = nscr()
                    p.op("dve", lambda e, i3=i3, i5=i5, i6=i6, w0c=w0c: e.scalar_tensor_tensor(scr[:, i6], scr[:, i3], w0c, scr[:, i5], ALU.mult, ALU.add),
                         reads=[scr_b[i3], scr_b[i5], cst_b], writes=[scr_b[i6]])
                    i7 = nscr()
                    p.dma("sp", scr[:, i7], bT_d[rows, tsl], reads=[], writes=[scr_b[i7]])
                    p.op("dve", lambda e, o=o, i6=i6, i7=i7: e.tensor_tensor(aT[:, o], scr[:, i6], scr[:, i7], ALU.mult),
                         reads=[scr_b[i6], scr_b[i7]], writes=[u_b[o]])
            if pro != "x":
                mm_fm(wout_d, 0, NCH, 16, lambda k: aT[:, k], u_b, resid_epilogue)
                if not p.dry:
                    ln_apply(0, 1)
                mm_fm(w1_d, 0, 64, 16, lambda k: hb[:, k], hb_b, relu2_epilogue)
                mm_fm(w2_d, 0, NCH, 64, lambda k: u[:, k], u_b, resid_epilogue)
                if not p.dry:
                    ln_apply(2, 3)
                    for o in range(NCH):
                        store_fm(hout, o, t, S[:, o], S_b[o], "hout")
            def fm_out(dst_d, name, dt_bf=False, func=None, scale=None):
                def ep(o, ps, psb):
                    i1 = nscr()
                    dst = scr_bf(i1) if dt_bf else scr[:, i1]
                    evac(dst, ps[:], [psb], [scr_b[i1]], func=func, scale=scale)
                    store_fm(dst_d, o, t, dst, scr_b[i1], name)
                return ep

            def tm_out(dst_d, name, dt_bf):
                def ep(cb, tb, ps, psb):
                    i1 = nscr()
                    dst = (scr_bf(i1) if dt_bf else scr[:, i1])[:, 0:256]
                    evac(dst, ps[:, 0:256], [psb], [scr_b[i1]])
                    p.dma("sp", dst_d[t * TT + tb * 128:t * TT + (tb + 1) * 128, cb * 256:(cb + 1) * 256], dst,
                          reads=[scr_b[i1]], writes=[obuf(name)])
                return ep

            if epi == "A":
                mm_fm(wmix_d, 0, NCH, 16, lambda k: hb[:, k], hb_b, fm_out(qT_o, "qT"))
                mm_fm(wmix_d, D, NCH, 16, lambda k: hb[:, k], hb_b, fm_out(fT_o, "fT"))
                mm_fm(wmix_d, 3 * D, NCH, 16, lambda k: hb[:, k], hb_b, fm_out(gs_o, "gs", func=AF.Silu))
                mm_tm(wmix_d, 2 * D, D, tm_out(itok_o, "itok", True))
            elif epi in ("B", "C"):
                mm_fm(wmix_d, 0, NCH, 16, lambda k: hb[:, k], hb_b, fm_out(qT_o, "qT", dt_bf=True, scale=1.0 / 16.0))
                mm_fm(wmix_d, D, NCH, 16, lambda k: hb[:, k], hb_b, fm_out(kT_o, "kT", dt_bf=True))
                mm_tm(wmix_d, 2 * D, D, tm_out(v_o, "v", True))
                if epi == "C":
                    slot, sbuf = ws.next(wmix_d, 0, 3 * D, 8)
                    if not p.dry:
                        b = nps()
                        for k in range(16):
                            p.mm(pst[b][0:8, :], slot[:, k, 0:8], hb[:, k], start=(k == 0), stop=(k == 15),
                                 reads=[sbuf, hb_b[k]], writes=[ps_b[b]])
                        i1 = nscr()
                        p.act(scr[0:8, i1], pst[b][0:8, :], AF.Copy, reads=[ps_b[b]], writes=[scr_b[i1]])
                        p.dma("sp", fl_o[:, tsl], scr[0:8, i1], reads=[scr_b[i1]], writes=[obuf("fl")])
            elif epi == "D":
                mm_fm(wmix_d, 0, NCH, 16, lambda k: hb[:, k], hb_b, fm_out(bT_o, "bT"))
                def cstage_v(o):
                    return u[:, 2 * o:2 * o + 2, :].bitcast(F32).rearrange("p a b -> p (a b)")
                def c_ep(o, ps, psb):
                    evac(cstage_v(o), ps[:], [psb], [u_b[2 * o], u_b[2 * o + 1]])
                mm_fm(wmix_d, D, NCH, 16, lambda k: hb[:, k], hb_b, c_ep)
                def h_ep(o, ps, psb):
                    i1 = nscr()
                    p.op("dve", lambda e: e.tensor_tensor(scr[:, i1], ps[:], cstage_v(o), ALU.mult),
                         reads=[psb, u_b[2 * o], u_b[2 * o + 1]], writes=[scr_b[i1]])
                    store_fm(uT_o, o, t, scr[:, i1], scr_b[i1], "uT")
                mm_fm(wmix_d, 2 * D, NCH, 16, lambda k: hb[:, k], hb_b, h_ep)
        if not p.dry:
            p.wait_all("sp", list(out_bufs.values()))

    p.dry = True
    record()
    p.dry = False
    state.update(scr=0, ps=0, ev=0)
    record()
    p.finish()
    return p
NEG = -30000.0


def build_attn(nc, kind, S=16384):
    p = Prog(nc)
    NB = S // 128
    NQT = S // 512
    fox = kind == "fox"

    def din(name, shape, dt=F32):
        return nc.dram_tensor(name, list(shape), dt, kind="ExternalInput").ap()

    qT_d = din("qT", [256, S], BF16)
    kT_d = din("kT", [256, S], BF16)
    v_d = din("v", [S, 256], BF16)
    mask_d = din("mask", [128, 128])
    ident_d = din("ident", [128, 128])
    o_d = nc.dram_tensor("o", [S, 256], BF16, kind="ExternalOutput").ap()
    if fox:
        fl_d = din("fl", [NB, 128])
        bf_d = din("bf", [128, 1])
        ltri_d = din("ltri", [128, 128])
        augk_d = din("augk", [128, 128])
        caug_d = nc.dram_tensor("caug_scr", [3, S], BF16, kind="Internal").ap()
    else:
        utri_d = din("utri", [128, 128])
        nones_d = din("nones", [128, 128])

    VW = 257 if fox else 256
    kT = p.sb("kT", [128, 2, S], BF16); kT_b = p.buf("kT")
    v = p.sb("v", [128, NB, VW], BF16); v_b = p.buf("v")
    NQ = 2
    qt = p.sb("qt", [128, NQ, 2, 512], BF16); qt_b = p.bufs(NQ, "qt")
    mask = p.sb("mask", [128, 128], BF16); ident = p.sb("ident", [128, 128], BF16); c_b = p.buf("consts")
    NP = 3
    pT = p.sb("pT", [128, NP, 512], BF16); pT_b = p.bufs(NP, "pT")
    NO = 4
    ost = p.sb("ost", [128, NO, 256], BF16); ost_b = p.bufs(NO, "ost")
    rc = p.sb("rc", [128, NO], F32); rc_b = p.bufs(NO, "rc")
    NSC = 4 if not fox else 3
    sc = [p.ps(f"sc{i}", [128, 512]) for i in range(NSC)]; sc_b = p.bufs(NSC, "sc")
    oacc = [p.ps(f"oacc{i}", [128, 512]) for i in range(4)]; oacc_b = p.bufs(4, "oacc")
    out_b = p.buf("out")

    p.dma("pool", mask[:], mask_d[:, :], reads=[], writes=[c_b])
    p.dma("pool", ident[:], ident_d[:, :], reads=[], writes=[c_b])
    for c in range(2):
        for g in range(4):
            sl = slice(g * S // 4, (g + 1) * S // 4)
            p.dma("sp", kT[:, c, sl], kT_d[c * 128:(c + 1) * 128, sl], reads=[], writes=[kT_b])
    vv = v_d.rearrange("(b s) d -> s b d", s=128)
    for g in range(4):
        bs = slice(g * NB // 4, (g + 1) * NB // 4)
        p.dma("sp", v[:, bs, 0:256], vv[:, bs, :], reads=[], writes=[v_b])

    if fox:
        p.op("dve", lambda e: e.memset(v[:, :, 256:257], 1.0), reads=[], writes=[v_b])
        ltri = p.sb("ltri", [128, 128], F32); identf = p.sb("identf", [128, 128], F32)
        augk = p.sb("augk", [128, 128], BF16)
        p.dma("sp", ltri[:], ltri_d[:, :], reads=[], writes=[c_b])
        p.dma("sp", identf[:], ident_d[:, :], reads=[], writes=[c_b])
        p.dma("pool", augk[:], augk_d[:, :], reads=[], writes=[c_b])
        bfc = p.sb("bfc", [128, 2], F32); bfc_b = p.buf("bfc")
        p.dma("sp", bfc[:, 0:1], bf_d[:, :], reads=[], writes=[bfc_b])
        fa = p.sb("fa", [128, 2, 128], F32); fa_b = p.bufs(2, "fa")
        fb = p.sb("fb", [128, 128], F32); fb_b = p.buf("fb")
        cn = p.sb("cn", [128, 128], F32); cn_b = p.buf("cn")
        cnT = p.sb("cnT", [128, 128], F32); cnT_b = p.buf("cnT")
        pc = p.sb("pc", [128, 3, 128], BF16); pc_b = p.bufs(3, "pc")
        rr = p.sb("rr", [128, 2, 128], F32); rr_b = p.bufs(2, "rr")
        qaug = p.sb("qaug", [128, NQ, 512], BF16); qaug_b = p.bufs(NQ, "qaug")
        nb_ = NB
        p.dma("sp", fa[0:nb_, 0], fl_d[:, :], reads=[], writes=[fa_b[0]])
        p.op("dve", lambda e: e.tensor_scalar(bfc[:, 1:2], bfc[:, 0:1], -1.0, None, ALU.mult), reads=[bfc_b], writes=[bfc_b])
        p.act(fa[0:nb_, 1], fa[0:nb_, 0], AF.Exp, reads=[fa_b[0], bfc_b], writes=[fa_b[1]], scale=-1.0, bias=bfc[0:nb_, 1:2])
        p.act(fa[0:nb_, 0], fa[0:nb_, 1], AF.Ln, reads=[fa_b[1]], writes=[fa_b[0]], bias=1.0)
        cur = 0
        d = 1
        while d < 128:
            nxt = 1 - cur
            p.op("dve", lambda e, cur=cur, nxt=nxt, d=d: e.tensor_tensor(fa[0:nb_, nxt, d:128], fa[0:nb_, cur, d:128], fa[0:nb_, cur, 0:128 - d], ALU.add),
                 reads=[fa_b[cur]], writes=[fa_b[nxt]])
            p.op("dve", lambda e, cur=cur, nxt=nxt, d=d: e.tensor_copy(fa[0:nb_, nxt, 0:d], fa[0:nb_, cur, 0:d]),
                 reads=[fa_b[cur]], writes=[fa_b[nxt]])
            cur = nxt
            d *= 2
        p.mm(sc[0][0:nb_, 0:1], ltri[0:nb_, 0:nb_], fa[0:nb_, cur, 127:128], True, True,
             reads=[c_b, fa_b[cur]], writes=[sc_b[0]])
        p.op("dve", lambda e: e.tensor_copy(fb[0:nb_, 0:1], sc[0][0:nb_, 0:1]), reads=[sc_b[0]], writes=[fb_b])
        if nb_ < 128:
            p.op("dve", lambda e: e.memset(cn[:], 0.0), reads=[], writes=[cn_b])
        p.op("dve", lambda e, cur=cur: e.tensor_scalar(cn[0:nb_, :], fa[0:nb_, cur, :], fb[0:nb_, 0:1], None, ALU.add),
             reads=[fa_b[cur], fb_b], writes=[cn_b])
        p.mm(sc[1][:, 0:128], cn[:], identf[:], True, True, reads=[cn_b, c_b], writes=[sc_b[1]])
        p.op("dve", lambda e: e.tensor_copy(cnT[:], sc[1][:, 0:128]), reads=[sc_b[1]], writes=[cnT_b])
        p.op("dve", lambda e: e.tensor_scalar(rr[:, 0], cn[:], -1.0, None, ALU.mult), reads=[cn_b], writes=[rr_b[0]])
        p.act(pc[:, 0], rr[:, 0], AF.Copy, reads=[rr_b[0]], writes=[pc_b[0]])
        p.op("dve", lambda e: e.tensor_tensor(rr[:, 1], rr[:, 0], pc[:, 0], ALU.subtract), reads=[rr_b[0], pc_b[0]], writes=[rr_b[1]])
        p.act(pc[:, 1], rr[:, 1], AF.Copy, reads=[rr_b[1]], writes=[pc_b[1]])
        p.op("dve", lambda e: e.tensor_tensor(rr[:, 0], rr[:, 1], pc[:, 1], ALU.subtract), reads=[rr_b[1], pc_b[1], rr_b[0]], writes=[rr_b[0]])
        p.act(pc[:, 2], rr[:, 0], AF.Copy, reads=[rr_b[0]], writes=[pc_b[2]])
        caug_b = p.buf("caug")
        for i in range(3):
            p.dma("sp", caug_d[i:i + 1, :].rearrange("o (b s) -> (o b) s", s=128), pc[0:nb_, i], reads=[pc_b[i]], writes=[caug_b])
        for i in range(NQ):
            p.op("dve", lambda e, i=i: e.memset(qaug[:, i], 0.0), reads=[], writes=[qaug_b[i]])
    else:
        utri = p.sb("utri", [128, 128], BF16); nones = p.sb("nones", [128, 128], BF16)
        p.dma("pool", utri[:], utri_d[:, :], reads=[], writes=[c_b])
        p.dma("pool", nones[:], nones_d[:, :], reads=[], writes=[c_b])
        NE = 2
        e1 = p.sb("e1", [128, NE, 512], F32); e1_b = p.bufs(NE, "e1")
        NS = 4
        sp = p.sb("sp", [128, NS, 512], BF16); sp_b = p.bufs(NS, "sp")
        R32 = p.sb("R32", [128, 512], F32); R32_b = p.buf("R32")
        NR = 3
        Rb = p.sb("Rb", [128, NR, 512], BF16); Rb_b = p.bufs(NR, "Rb")

    cnt = dict(sc=0, pT=0, o=0, sp=0, e1=0, rb=0)

    def rot(key, n):
        i = cnt[key] % n
        cnt[key] += 1
        return i

    def finalize(i, ob):
        oi = rot("o", NO)
        if fox:
            p.op("dve", lambda e: e.reciprocal(rc[:, oi:oi + 1], oacc[ob][:, 256:257]), reads=[oacc_b[ob]], writes=[rc_b[oi]])
            p.act(ost[:, oi], oacc[ob][:, 0:256], AF.Copy, reads=[oacc_b[ob], rc_b[oi]], writes=[ost_b[oi]], scale=rc[:, oi:oi + 1])
        else:
            p.act(ost[:, oi], oacc[ob][:, 0:256], AF.Copy, reads=[oacc_b[ob]], writes=[ost_b[oi]])
        p.dma("sp", o_d[i * 128:(i + 1) * 128, :], ost[:, oi], reads=[ost_b[oi]], writes=[out_b])

    def pv(T, j, pi, c0):
        for i in range(max(j, 4 * T), 4 * T + 4):
            cols = (i - 4 * T) * 128
            ob = i % 4
            if fox:
                st_, sp_ = (j == 0), (j == i)
            else:
                st_, sp_ = (j == i), (j == 0)
            p.mm(oacc[ob][:, 0:VW], pT[:, pi, cols:cols + 128], v[:, j, :], st_, sp_,
                 reads=[pT_b[pi], v_b], writes=[oacc_b[ob]])
            if sp_:
                finalize(i, ob)

    for T in range(NQT):
        qi = T % NQ
        for c in range(2):
            p.dma("sp", qt[:, qi, c, :], qT_d[c * 128:(c + 1) * 128, T * 512:(T + 1) * 512], reads=[], writes=[qt_b[qi]])
        if fox:
            for i in range(3):
                p.dma("sp", qaug[32 * i:32 * i + 1, qi, :], caug_d[i:i + 1, T * 512:(T + 1) * 512], reads=[caug_b], writes=[qaug_b[qi]])
        nj = 4 * T + 4
        order = list(range(nj)) if fox else list(range(nj - 1, -1, -1))
        pend = []
        if not fox:
            first = [True]
        for idx, j in enumerate(order):
            c0 = max(0, (j - 4 * T)) * 128
            si = rot("sc", NSC)
            w = 512 - c0
            diag = j >= 4 * T
            p.mm(sc[si][:, c0:512], kT[:, 0, j * 128:(j + 1) * 128], qt[:, qi, 0, c0:512], True, False,
                 reads=[kT_b, qt_b[qi]], writes=[sc_b[si]])
            last_qk = False
            p.mm(sc[si][:, c0:512], kT[:, 1, j * 128:(j + 1) * 128], qt[:, qi, 1, c0:512], False, last_qk,
                 reads=[kT_b, qt_b[qi]], writes=[sc_b[si]])
            if fox:
                p.mm(sc[si][:, c0:512], augk[:], qaug[:, qi, c0:512], False, not diag,
                     reads=[c_b, qaug_b[qi]], writes=[sc_b[si]])
            if diag:
                p.mm(sc[si][:, c0:c0 + 128], ident[:], mask[:], False, fox, reads=[c_b], writes=[sc_b[si]])
            if fox:
                pi = rot("pT", NP)
                p.act(pT[:, pi, c0:512], sc[si][:, c0:512], AF.Exp, reads=[sc_b[si], cnT_b], writes=[pT_b[pi]],
                      bias=cnT[:, j:j + 1])
                for f in pend:
                    f()
                pend = [lambda T=T, j=j, pi=pi, c0=c0: pv(T, j, pi, c0)]
            else:
                ei = rot("e1", NE); spi = rot("sp", NS)
                p.act(e1[:, ei, c0:512], sc[si][:, c0:512], AF.Exp, reads=[sc_b[si]], writes=[e1_b[ei]])
                p.act(sp[:, spi, c0:512], e1[:, ei, c0:512], AF.Ln, reads=[e1_b[ei]], writes=[sp_b[spi]], bias=1.0)
                ri = None
                if idx > 0:
                    ri = (cnt["rb"] - 1) % NR
                def stageB(T=T, j=j, si=si, spi=spi, c0=c0, ri=ri, idx=idx):
                    p.mm(sc[si][:, c0:512], utri[:], sp[:, spi, c0:512], False, idx == 0,
                         reads=[c_b, sp_b[spi]], writes=[sc_b[si]])
                    if idx > 0:
                        p.mm(sc[si][:, c0:512], nones[:], Rb[:, ri, c0:512], False, True,
                             reads=[c_b, Rb_b[ri]], writes=[sc_b[si]])
                    pi = rot("pT", NP)
                    p.act(pT[:, pi, c0:512], sc[si][:, c0:512], AF.Exp, reads=[sc_b[si]], writes=[pT_b[pi]])
                    return lambda: pv(T, j, pi, c0)
                if idx == 0:
                    if c0 > 0:
                        p.op("dve", lambda e, c0=c0: e.memset(R32[:, 0:c0], 0.0), reads=[], writes=[R32_b])
                    p.op("dve", lambda e, spi=spi, c0=c0: e.tensor_copy(R32[:, c0:512], sp[:, spi, c0:512]), reads=[sp_b[spi]], writes=[R32_b])
                else:
                    p.op("dve", lambda e, spi=spi, c0=c0: e.tensor_tensor(R32[:, c0:512], R32[:, c0:512], sp[:, spi, c0:512], ALU.add),
                         reads=[sp_b[spi], R32_b], writes=[R32_b])
                rn = rot("rb", NR)
                p.op("dve", lambda e, rn=rn: e.tensor_copy(Rb[:, rn], R32[:]), reads=[R32_b], writes=[Rb_b[rn]])
                newp = []
                for f in pend:
                    r = f()
                    if r is not None:
                        newp.append(r)
                pend = newp + [stageB]
        while pend:
            newp = []
            for f in pend:
                r = f()
                if r is not None:
                    newp.append(r)
            pend = newp
    p.wait_all("sp", [out_b])
    p.finish()
    return p


def build_hgrn(nc, S=16384):
    p = Prog(nc)
    NT = S // 512
    NB = S // 128

    def din(name, shape, dt=F32):
        return nc.dram_tensor(name, list(shape), dt, kind="ExternalInput").ap()

    zq_d = din("zqT", [256, S]); zf_d = din("zfT", [256, S])
    v_d = din("vtok", [S, 256], BF16)
    lbl_d = din("lbl", [128, 10])
    ng_d = din("ng", [128, 1])
    ident_d = din("ident", [128, 128]); ones_d = din("ones", [128, 128])
    cmask_d = din("cmask", [128, 128]); rmask_d = din("rmask", [128, 4])
    on_d = nc.dram_tensor("on", [256, S], F32, kind="ExternalOutput").ap()

    c_b = p.buf("consts")
    ident = p.sb("ident", [128, 128], BF16); ones = p.sb("ones", [128, 128], BF16)
    cmask = p.sb("cmask", [128, 128], F32); rmask = p.sb("rmask", [128, 4], F32)
    ng = p.sb("ng", [128, 1], F32)
    lbl = p.sb("lbl", [128, 2, 5], F32); lbe = p.sb("lbe", [128, 2, 5], F32); lbs = p.sb("lbs", [128, 8], F32)
    lb_b = p.buf("lb")
    p.dma("pool", ident[:], ident_d[:, :], reads=[], writes=[c_b])
    p.dma("pool", ones[:], ones_d[:, :], reads=[], writes=[c_b])
    p.dma("sp", cmask[:], cmask_d[:, :], reads=[], writes=[c_b])
    p.dma("sp", rmask[:], rmask_d[:, :], reads=[], writes=[c_b])
    p.dma("sp", ng[:], ng_d[:, :], reads=[], writes=[c_b])
    p.dma("sp", lbl[:], lbl_d.rearrange("p (h r) -> p h r", r=5), reads=[], writes=[lb_b])
    p.act(lbe[:], lbl[:], AF.Exp, reads=[lb_b], writes=[lb_b])
    p.op("dve", lambda e: e.reduce_sum(lbs[:, 0:2], lbe[:], AX.X), reads=[lb_b], writes=[lb_b])
    p.op("dve", lambda e: e.reciprocal(lbs[:, 2:4], lbs[:, 0:2]), reads=[lb_b], writes=[lb_b])
    p.op("dve", lambda e: e.tensor_tensor(lbs[:, 4:6], lbe[:, :, 0], lbs[:, 2:4], ALU.mult), reads=[lb_b], writes=[lb_b])
    p.op("dve", lambda e: e.tensor_scalar(lbs[:, 6:8], lbs[:, 4:6], -1.0, 1.0, ALU.mult, ALU.add), reads=[lb_b], writes=[lb_b])

    v = p.sb("v", [128, NB, 256], BF16); v_b = p.buf("v")
    vv = v_d.rearrange("(b s) d -> s b d", s=128)
    for g in range(4):
        bs = slice(g * NB // 4, (g + 1) * NB // 4)
        p.dma("sp", v[:, bs, :], vv[:, bs, :], reads=[], writes=[v_b])

    NF = 10
    ft = [p.sb(f"ft{h}", [128, NF, 512], F32) for h in range(2)]
    ft_b = [p.bufs(NF, f"ft{h}_") for h in range(2)]
    bt = [p.sb(f"bt{h}", [128, 3, 512], BF16) for h in range(2)]
    bt_b = [p.bufs(3, f"bt{h}_") for h in range(2)]
    S32 = [p.sb(f"S32_{h}", [128, 128], F32) for h in range(2)]; S32_b = p.bufs(2, "S32")
    Sb = [p.sb(f"Sb_{h}", [128, 2, 128], BF16) for h in range(2)]; Sb_b = [p.bufs(2, f"Sb{h}_") for h in range(2)]
    scm = p.sb("scm", [128, 2, 128], BF16); scm_b = p.bufs(2, "scm")
    kdm = p.sb("kdm", [128, 2, 4, 128], BF16); kdm_b = p.bufs(2, "kdm")
    o32 = p.sb("o32", [128, 2, 128], F32); o32_b = p.bufs(2, "o32")
    osq = p.sb("osq", [128, 2, 128], BF16); osq_b = p.bufs(2, "osq")
    rt = p.sb("rt", [128, 2, 128], F32); rt_b = p.bufs(2, "rt")
    onst = p.sb("onst", [128, 2, 512], F32); onst_b = p.bufs(2, "onst")
    pA = [p.ps(f"pA{i}", [128, 128]) for i in range(2)]; pA_b = p.bufs(2, "pA")
    pT = p.ps("pT", [128, 128], BF16); pT_b = p.buf("pT")
    pO = [p.ps(f"pO{i}", [128, 128]) for i in range(2)]; pO_b = p.bufs(2, "pO")
    pD = [p.ps(f"pD{i}", [128, 128]) for i in range(2)]; pD_b = p.bufs(2, "pD")
    pM = p.ps("pM", [128, 128]); pM_b = p.buf("pM")
    out_b = p.buf("out")
    cnt = dict(a=0, d=0, o=0, x=0)

    def rot(k, n):
        i = cnt[k] % n
        cnt[k] += 1
        return i

    for h in range(2):
        p.op("dve", lambda e, h=h: e.memset(S32[h][:], 0.0), reads=[], writes=[S32_b[h]])
        p.op("dve", lambda e, h=h: e.memset(Sb[h][:], 0.0), reads=[], writes=[Sb_b[h][0], Sb_b[h][1]])
    sbi = [0, 0]
    V = "dve"
    for t in range(NT):
        tsl = slice(t * 512, (t + 1) * 512)
        for h in range(2):
            F = ft[h]; Fb = ft_b[h]; B = bt[h]; Bb = bt_b[h]
            lbc = lbs[:, 4 + h:5 + h]; omlc = lbs[:, 6 + h:7 + h]
            ZQ, ZF, FF, KK, SA, SBB, EBC, ENB, QS, KI = range(10)
            p.dma("sp", F[:, ZQ], zq_d[h * 128:(h + 1) * 128, tsl], reads=[], writes=[Fb[ZQ]])
            p.dma("sp", F[:, ZF], zf_d[h * 128:(h + 1) * 128, tsl], reads=[], writes=[Fb[ZF]])
            p.act(F[:, FF], F[:, ZF], AF.Sigmoid, reads=[Fb[ZF]], writes=[Fb[FF]])
            p.op(V, lambda e, F=F, lbc=lbc, omlc=omlc: e.tensor_scalar(F[:, FF], F[:, FF], omlc, lbc, ALU.mult, ALU.add),
                 reads=[Fb[FF], lb_b], writes=[Fb[FF]])
            p.act(F[:, SA], F[:, FF], AF.Ln, reads=[Fb[FF]], writes=[Fb[SA]])
            p.op(V, lambda e, F=F: e.tensor_scalar(F[:, KK], F[:, FF], -1.0, 1.0, ALU.mult, ALU.add), reads=[Fb[FF]], writes=[Fb[KK]])
            cur, nxt = SA, SBB
            d = 1
            while d < 16:
                xv = F[:, cur].rearrange("p (c j) -> p c j", j=16)
                yv = F[:, nxt].rearrange("p (c j) -> p c j", j=16)
                p.op(V, lambda e, xv=xv, yv=yv, d=d: e.tensor_tensor(yv[:, :, d:16], xv[:, :, d:16], xv[:, :, 0:16 - d], ALU.add),
                     reads=[Fb[cur]], writes=[Fb[nxt]])
                p.op(V, lambda e, xv=xv, yv=yv, d=d: e.tensor_copy(yv[:, :, 0:d], xv[:, :, 0:d]), reads=[Fb[cur]], writes=[Fb[nxt]])
                cur, nxt = nxt, cur
                d *= 2
            BC = cur
            p.act(F[:, EBC], F[:, BC], AF.Exp, reads=[Fb[BC]], writes=[Fb[EBC]])
            p.act(F[:, ENB], F[:, BC], AF.Exp, reads=[Fb[BC]], writes=[Fb[ENB]], scale=-1.0)
            p.act(F[:, QS], F[:, ZQ], AF.Silu, reads=[Fb[ZQ]], writes=[Fb[QS]])
            p.op(V, lambda e, F=F, B=B: e.scalar_tensor_tensor(B[:, 0], F[:, QS], 128.0 ** -0.5, F[:, EBC], ALU.mult, ALU.mult),
                 reads=[Fb[QS], Fb[EBC]], writes=[Bb[0]])
            p.op(V, lambda e, F=F: e.tensor_tensor(F[:, KI], F[:, KK], F[:, ENB], ALU.mult), reads=[Fb[KK], Fb[ENB]], writes=[Fb[KI]])
            p.act(B[:, 1], F[:, KI], AF.Copy, reads=[Fb[KI]], writes=[Bb[1]])
            dlv = F[:, EBC].rearrange("p (c j) -> p c j", j=16)[:, :, 15:16].to_broadcast([128, 32, 16])
            p.op(V, lambda e, F=F, B=B, dlv=dlv: e.tensor_tensor(B[:, 2].rearrange("p (c j) -> p c j", j=16),
                                                            F[:, KI].rearrange("p (c j) -> p c j", j=16), dlv, ALU.mult),
                 reads=[Fb[KI], Fb[EBC]], writes=[Bb[2]])
            oi = rot("o", 2)
            for b in range(4):
                blk = t * 4 + b
                bs = slice(b * 128, (b + 1) * 128)
                vh = slice(h * 128, (h + 1) * 128)
                ai = rot("a", 2)
                p.mm(pA[ai][:], B[:, 1, bs], B[:, 0, bs], True, True, reads=[Bb[1], Bb[0]], writes=[pA_b[ai]])
                p.op(V, lambda e, ai=ai: e.tensor_tensor(scm[:, ai], pA[ai][:], cmask[:], ALU.mult), reads=[pA_b[ai], c_b], writes=[scm_b[ai]])
                p.op("pe", lambda e, B=B, bs=bs: e.transpose(pT[:], B[:, 2, bs], ident[:]), reads=[Bb[2], c_b], writes=[pT_b])
                for q in range(4):
                    if q % 2 == 0:
                        p.act(kdm[:, ai, q], pT[:], AF.Copy, reads=[pT_b, c_b], writes=[kdm_b[ai]], scale=rmask[:, q:q + 1])
                    else:
                        p.op(V, lambda e, ai=ai, q=q: e.tensor_scalar(kdm[:, ai, q], pT[:], rmask[:, q:q + 1], None, ALU.mult),
                             reads=[pT_b, c_b], writes=[kdm_b[ai]])
                po = rot("x", 2)
                p.mm(pO[po][:], v[:, blk, vh], scm[:, ai], True, False, reads=[v_b, scm_b[ai]], writes=[pO_b[po]])
                for n in range(8):
                    col = b * 128 + n * 16
                    si = sbi[h]
                    p.mm(pO[po][:, n * 16:(n + 1) * 16], Sb[h][:, si], B[:, 0, col:col + 16], False, n == 7,
                         reads=[Sb_b[h][si], Bb[0]], writes=[pO_b[po]])
                    di = rot("d", 2)
                    half = slice(64 * (n // 4), 64 * (n // 4) + 64)
                    p.mm(pD[di][:], kdm[half, ai, n % 4], v[half, blk, vh], True, True, reads=[kdm_b[ai], v_b], writes=[pD_b[di]])
                    p.op(V, lambda e, h=h, di=di, F=F, col=col: e.scalar_tensor_tensor(S32[h][:], S32[h][:], F[:, EBC, col + 15:col + 16], pD[di][:], ALU.mult, ALU.add),
                         reads=[S32_b[h], Fb[EBC], pD_b[di]], writes=[S32_b[h]])
                    sn = 1 - si
                    p.act(Sb[h][:, sn], S32[h][:], AF.Copy, reads=[S32_b[h]], writes=[Sb_b[h][sn]])
                    sbi[h] = sn
                p.act(o32[:, ai], pO[po][:], AF.Copy, reads=[pO_b[po]], writes=[o32_b[ai]])
                p.act(osq[:, ai], pO[po][:], AF.Square, reads=[pO_b[po]], writes=[osq_b[ai]])
                p.mm(pM[:], ones[:], osq[:, ai], True, True, reads=[c_b, osq_b[ai]], writes=[pM_b])
                p.op(V, lambda e, ai=ai: e.tensor_scalar(rt[:, ai], pM[:], 1.0 / 128.0, 1e-6, ALU.mult, ALU.add), reads=[pM_b], writes=[rt_b[ai]])
                p.act(rt[:, ai], rt[:, ai], AF.Sqrt, reads=[rt_b[ai]], writes=[rt_b[ai]])
                p.op(V, lambda e, ai=ai: e.reciprocal(rt[:, ai], rt[:, ai]), reads=[rt_b[ai]], writes=[rt_b[ai]])
                p.op(V, lambda e, ai=ai, oi=oi, bs=bs: e.scalar_tensor_tensor(onst[:, oi, bs], o32[:, ai], ng[:, 0:1], rt[:, ai], ALU.mult, ALU.mult),
                     reads=[o32_b[ai], rt_b[ai], c_b], writes=[onst_b[oi]])
            p.dma("sp", on_d[h * 128:(h + 1) * 128, tsl], onst[:, oi], reads=[onst_b[oi]], writes=[out_b])
    p.wait_all("sp", [out_b])
    p.finish()
    return p
import ml_dtypes as _mld

_BF = _mld.bfloat16
NCORE = 8
SEQ = 16384
TOK = SEQ // NCORE
_cache = {}


def _prog(key, builder):
    if key not in _cache:
        nc = bass.Bass("TRN2", target_bir_lowering=False)
        builder(nc)
        _cache[key] = nc
    return _cache[key]


def _run(nc, in_maps):
    res = run_bass_kernel_spmd(nc, in_maps, core_ids=list(range(NCORE)))
    return res.results


def _pp(vv):
    return np.ascontiguousarray(np.asarray(vv, np.float32).reshape(16, 128).T)


def _lnp(g1, b1, g2, b2):
    return np.ascontiguousarray(np.concatenate([_pp(g1), _pp(b1), _pp(g2), _pp(b2)], axis=1))


def _heads_fm(outs, name, c, w=256):
    return np.ascontiguousarray(np.concatenate([o[name][c * w:(c + 1) * w, :] for o in outs], axis=1))


def _heads_tm(outs, name, c, w=256):
    return np.ascontiguousarray(np.concatenate([o[name][:, c * w:(c + 1) * w] for o in outs], axis=0))


def kernel(x, w_mix_a, norm_g_a, lb_logits, w_out_a, w_mix_b, w_out_b, w_mix_c, b_f_c, w_out_c,
           w_mix_d, conv_w_d, w_out_d, ln_mix_g, ln_mix_b, w_ff1, w_ff2, ln_ff_g, ln_ff_b):
    f32 = np.float32
    x = np.asarray(x, f32)[0]
    ones = np.ones((128, 128), f32)
    ident = np.eye(128, dtype=f32)
    tt = np.arange(128)
    xT = [np.ascontiguousarray(x[c * TOK:(c + 1) * TOK].T) for c in range(NCORE)]

    def ffw(i):
        return dict(w1=np.asarray(w_ff1[i], f32), w2=np.asarray(w_ff2[i], f32),
                    lnp=_lnp(ln_mix_g[i], ln_mix_b[i], ln_ff_g[i], ln_ff_b[i]), ones=ones)

    nc = _prog("row_x_A", lambda nc: build_row(nc, "x", "A", TOK))
    wa = np.asarray(w_mix_a[0], f32)
    r1 = _run(nc, [dict(hin=xT[c], w_mix=wa, ones=ones) for c in range(NCORE)])
    nc = _prog("hgrn", lambda nc: build_hgrn(nc, SEQ))
    lbl = np.asarray(lb_logits, f32).reshape(5, 16, 128)
    cm = ((tt[:, None] // 16 == tt[None, :] // 16) & (tt[:, None] <= tt[None, :])).astype(f32)
    rm = np.stack([((tt // 16) % 4 == q).astype(f32) for q in range(4)], 1)
    ng = np.ascontiguousarray(np.asarray(norm_g_a[0], f32).reshape(128, 1))
    r2 = _run(nc, [dict(zqT=_heads_fm(r1, "qT", c), zfT=_heads_fm(r1, "fT", c), vtok=_heads_tm(r1, "itok", c),
                        lbl=np.ascontiguousarray(lbl[:, 2 * c:2 * c + 2, :].transpose(2, 1, 0).reshape(128, 10)),
                        ng=ng, ident=ident, ones=ones, cmask=cm, rmask=rm) for c in range(NCORE)])
    nc = _prog("row_A2_B", lambda nc: build_row(nc, "A2", "B", TOK))
    wb = np.asarray(w_mix_b[0], f32)
    wo = np.asarray(w_out_a[0], f32)
    f0 = ffw(0)
    r3 = _run(nc, [dict(hin=xT[c], gs=r1[c]["gs_o"],
                        on=np.ascontiguousarray(np.concatenate([r2[h]["on"][:, c * TOK:(c + 1) * TOK] for h in range(NCORE)], axis=0)),
                        w_out=wo, w_mix=wb, **f0) for c in range(NCORE)])
    del r1, r2
    nc = _prog("attn_sb", lambda nc: build_attn(nc, "sb", SEQ))
    msk_sb = np.where(tt[:, None] < tt[None, :], 0.0, NEG).astype(f32)
    utri = -(tt[:, None] >= tt[None, :]).astype(f32)
    r4 = _run(nc, [dict(qT=_heads_fm(r3, "qT", c), kT=_heads_fm(r3, "kT", c), v=_heads_tm(r3, "v", c),
                        mask=msk_sb, ident=ident, utri=utri, nones=-ones) for c in range(NCORE)])

    def oT_for(rr, c):
        return np.ascontiguousarray(np.concatenate([rr[h]["o"][c * TOK:(c + 1) * TOK, :].T for h in range(NCORE)], axis=0))

    nc = _prog("row_attn_C", lambda nc: build_row(nc, "attn", "C", TOK))
    wc = np.asarray(w_mix_c[0], f32)
    wo = np.asarray(w_out_b[0], f32)
    f1 = ffw(1)
    r5 = _run(nc, [dict(hin=r3[c]["hout"], oT=oT_for(r4, c), w_out=wo, w_mix=wc, **f1) for c in range(NCORE)])
    del r3, r4
    nc = _prog("attn_fox", lambda nc: build_attn(nc, "fox", SEQ))
    msk_fx = np.where(tt[:, None] <= tt[None, :], 0.0, NEG).astype(f32)
    ltri = (tt[:, None] < tt[None, :]).astype(f32)
    augk = np.zeros((128, 128), f32)
    augk[[0, 32, 64], :] = 1.0
    bfv = np.asarray(b_f_c[0], f32)
    r6 = _run(nc, [dict(qT=_heads_fm(r5, "qT", c), kT=_heads_fm(r5, "kT", c), v=_heads_tm(r5, "v", c),
                        mask=msk_fx, ident=ident, ltri=ltri, augk=augk,
                        fl=np.ascontiguousarray(np.concatenate([r5[cc]["fl"][c, :] for cc in range(NCORE)]).reshape(SEQ // 128, 128)),
                        bf=np.full((128, 1), bfv[c], f32)) for c in range(NCORE)])
    nc = _prog("row_attn_D", lambda nc: build_row(nc, "attn", "D", TOK))
    wd = np.asarray(w_mix_d[0], f32)
    wo = np.asarray(w_out_c[0], f32)
    f2 = ffw(2)
    r7 = _run(nc, [dict(hin=r5[c]["hout"], oT=oT_for(r6, c), w_out=wo, w_mix=wd, **f2) for c in range(NCORE)])
    del r5, r6
    nc = _prog("row_conv_final", lambda nc: build_row(nc, "conv", "final", TOK))
    uT = np.concatenate([np.zeros((2048, 2), f32)] + [r7[c]["uT_o"] for c in range(NCORE)], axis=1)
    cw = np.asarray(conv_w_d[0], f32)
    cwp = np.ascontiguousarray(np.concatenate([_pp(cw[0]), _pp(cw[1]), _pp(cw[2])], axis=1))
    wo = np.asarray(w_out_d[0], f32)
    f3 = ffw(3)
    r8 = _run(nc, [dict(hin=r7[c]["hout"], uT=np.ascontiguousarray(uT[:, c * TOK:c * TOK + TOK + 2]), bT=r7[c]["bT_o"],
                        cw=cwp, w_out=wo, **f3) for c in range(NCORE)])
    out = np.concatenate([np.ascontiguousarray(r8[c]["hout"].T) for c in range(NCORE)], axis=0)
    return out.astype(f32)[None]
```

```python
import contextlib
import numpy as np
import concourse.bass as bass
import concourse.mybir as mybir
from concourse.bass_utils import run_bass_kernel_spmd

F32 = mybir.dt.float32
BF16 = mybir.dt.bfloat16
AF = mybir.ActivationFunctionType
ALU = mybir.AluOpType
AX = mybir.AxisListType

SEM_CHUNK = 20000
SAME_ENGINE_SYNC = True
SEMVAL = {}


def sem_alloc(nc, name):
    h = nc.alloc_semaphore(name=name)
    return h, SEMVAL.get(h.num, 0)


def sem_free(nc, h, final_value):
    SEMVAL[h.num] = final_value
    nc.release_semaphore(h)


class Buf:
    __slots__ = ("name", "lastw", "readers", "dsems", "dcount")

    def __init__(self, name):
        self.name = name
        self.lastw = None
        self.readers = []
        self.dsems = []
        self.dcount = 0


class Prog:
    ENGS = ("pe", "act", "dve", "pool", "sp")

    def __init__(self, nc, tag=""):
        self.nc = nc
        self.tag = tag
        self.sems = []
        self.sem_base = {}
        self.sem_final = {}
        self.ops = []
        self.dry = False
        self.stack = contextlib.ExitStack()
        self.nbuf = 0

    def sb(self, name, shape, dt):
        return self.stack.enter_context(self.nc.sbuf_tensor("sb_" + self.tag + name, list(shape), dt))

    def ps(self, name, shape, dt=F32):
        return self.stack.enter_context(self.nc.psum_tensor("pm_" + self.tag + name, list(shape), dt))

    def buf(self, name=None):
        self.nbuf += 1
        return Buf(name or f"b{self.nbuf}")

    def bufs(self, n, name="b"):
        return [self.buf(f"{name}{i}") for i in range(n)]

    def op(self, eng, fn, reads=(), writes=(), dma=False):
        if self.dry:
            return
        self.ops.append(dict(eng=eng, fn=fn, reads=list(reads), writes=list(writes), dma=dma))

    def mm(self, out, lhsT, rhs, start, stop, reads, writes):
        self.op("pe", lambda e: e.matmul(out, lhsT, rhs, start=start, stop=stop), reads, writes)

    def act(self, out, in_, func, reads, writes, **kw):
        self.op("act", lambda e: e.activation(out=out, in_=in_, func=func, **kw), reads, writes)

    def dma(self, eng, out, in_, reads, writes, **kw):
        self.op(eng, lambda e: e.dma_start(out=out, in_=in_, **kw), reads, writes, dma=True)

    def wait_all(self, eng, bufs):
        self.op(eng, None, reads=list(bufs), writes=[])

    def _sem(self, name):
        h, base = sem_alloc(self.nc, self.tag + name)
        self.sems.append(h)
        self.sem_base[h.num] = base
        self.sem_final[h.num] = base
        return h

    def finish(self):
        nc = self.nc
        ops = self.ops
        for i, o in enumerate(ops):
            deps = set()
            for b in o["reads"]:
                if b.lastw is not None:
                    deps.add(b.lastw)
            for b in o["writes"]:
                if b.lastw is not None:
                    deps.add(b.lastw)
                deps.update(b.readers)
            for b in o["reads"]:
                b.readers.append(i)
            for b in o["writes"]:
                b.lastw = i
                b.readers = []
            deps.discard(i)
            fd = []
            for j in deps:
                oj = ops[j]
                if not oj["dma"]:
                    if oj["fn"] is None:
                        continue
                    if oj["eng"] == o["eng"]:
                        if o["eng"] == "pe" or not SAME_ENGINE_SYNC:
                            continue
                    oj["signal"] = True
                fd.append(j)
            o["deps"] = fd
            if o["dma"]:
                b = o["writes"][0]
                if b.dcount + 16 > SEM_CHUNK or not b.dsems:
                    b.dsems.append(self._sem(f"d_{b.name}_{len(b.dsems)}"))
                    b.dcount = 0
                b.dcount += 16
                hh = b.dsems[-1]
                o["dsem"] = (hh, self.sem_base[hh.num] + b.dcount)
                self.sem_final[hh.num] = self.sem_base[hh.num] + b.dcount
        cnt = {e: 0 for e in self.ENGS}
        esems = {e: [] for e in self.ENGS}
        for o in ops:
            if o.get("signal") and not o["dma"]:
                e = o["eng"]
                c = cnt[e]
                if c % SEM_CHUNK == 0:
                    esems[e].append(self._sem(f"s_{e}_{len(esems[e])}"))
                cnt[e] += 1
                hh = esems[e][-1]
                o["sig"] = (hh, self.sem_base[hh.num] + c % SEM_CHUNK + 1)
                self.sem_final[hh.num] = self.sem_base[hh.num] + c % SEM_CHUNK + 1
        per_eng = {e: [o for o in ops if o["eng"] == e] for e in self.ENGS}
        nwaits = {e: 0 for e in self.ENGS}

        def emit(engname, eng):
            waited = {}
            for o in per_eng[engname]:
                need = {}
                for j in o["deps"]:
                    oj = ops[j]
                    sem, val = oj["dsem"] if oj["dma"] else oj["sig"]
                    k = id(sem)
                    if k not in need or need[k][1] < val:
                        need[k] = (sem, val)
                for k, (sem, val) in need.items():
                    if waited.get(k, 0) >= val:
                        continue
                    eng.wait_ge(sem, val)
                    waited[k] = val
                    nwaits[engname] += 1
                if o["fn"] is None:
                    continue
                ins = o["fn"](eng)
                if o["dma"]:
                    ins.then_inc(o["dsem"][0], 16)
                elif o.get("signal"):
                    ins.then_inc(o["sig"][0], 1)

        with nc.Block() as block:
            @block.tensor
            def _(e):
                emit("pe", e)

            @block.scalar
            def _(e):
                emit("act", e)

            @block.vector
            def _(e):
                emit("dve", e)

            @block.gpsimd
            def _(e):
                emit("pool", e)

            @block.sync
            def _(e):
                emit("sp", e)
        self.stats = dict(n_ops={e: len(per_eng[e]) for e in self.ENGS}, n_waits=nwaits, n_sems=len(self.sems))
        self.stack.close()
        for h in self.sems:
            sem_free(nc, h, self.sem_final[h.num])
        self.sems = []
D = 2048
DFF = 8192
NCH = 16
TT = 512
ALPHA = (2.0 * 4) ** 0.25
LN_EPS = 1e-5
NWSLOT = 4


class WStream:
    def __init__(self, p):
        self.p = p
        self.plan = []
        self.pos = 0
        self.issued = 0
        self.slots = None
        self.sbufs = None

    def alloc(self):
        p = self.p
        self.slots = p.sb("wslots", [128, NWSLOT, 16, 256], BF16)
        self.sbufs = p.bufs(NWSLOT, "wslot")

    def _issue(self, i):
        W, k0, c0, bc = self.plan[i]
        s = i % NWSLOT
        if bc == 256:
            src = W[c0 // 256, k0 // 16]
            self.p.dma("pool", self.slots[:, s].rearrange("p k c -> p (k c)"), src, reads=[], writes=[self.sbufs[s]])
        else:
            src = W.rearrange("(k p) n -> p k n", p=128)[:, k0:k0 + 16, c0:c0 + bc]
            self.p.dma("pool", self.slots[:, s, :, 0:bc], src, reads=[], writes=[self.sbufs[s]])

    def next(self, W, k0, c0, bc=256):
        p = self.p
        if p.dry:
            self.plan.append((W, k0, c0, bc))
            return None, None
        i = self.pos
        self.pos += 1
        while self.issued < len(self.plan) and self.issued <= i + (NWSLOT - 1):
            self._issue(self.issued)
            self.issued += 1
        s = i % NWSLOT
        return self.slots[:, s], self.sbufs[s]


def build_row(nc, pro, epi, ntok=2048, io=None, tag=""):
    p = Prog(nc, tag)
    NT = ntok // TT

    def din(name, shape, dt=F32):
        if io is not None:
            ap = io[name]
            assert list(ap.shape) == list(shape) and ap.dtype == dt, (name, ap.shape, shape, ap.dtype, dt)
            return ap
        return nc.dram_tensor(name, list(shape), dt, kind="ExternalInput").ap()

    def dout(name, shape, dt=F32):
        if io is not None:
            ap = io[name]
            assert list(ap.shape) == list(shape) and ap.dtype == dt, (name, ap.shape, shape, ap.dtype, dt)
            return ap
        return nc.dram_tensor(name, list(shape), dt, kind="ExternalOutput").ap()

    hin = din("hin", [D, ntok])
    if pro == "A2":
        on_d = din("on", [D, ntok], BF16)
        gs_d = din("gs", [D, ntok])
    if pro == "attn":
        oT_d = din("oT", [D, ntok], BF16)
    if pro == "conv":
        uT_d = din("uT", [D, ntok + 2])
        bT_d = din("bT", [D, ntok])
        cw_d = din("cw", [128, 3 * NCH])
    if pro != "x":
        wout_d = din("w_out", [D // 256, 1, 128, 4096])
        w1_d = din("w1", [DFF // 256, 1, 128, 4096])
        w2_d = din("w2", [D // 256, 4, 128, 4096])
        lnp_d = din("lnp", [128, 4 * NCH])
    ones_d = din("ones", [128, 128])
    if epi == "A":
        wmix_d = din("w_mix", [4 * D // 256, 1, 128, 4096])
        qT_o = dout("qT", [D, ntok], BF16); fT_o = dout("fT", [D, ntok]); gs_o = dout("gs_o", [D, ntok])
        itok_o = dout("itok", [8, ntok, 256], BF16)
    elif epi in ("B", "C"):
        wmix_d = din("w_mix", [3 * D // 256, 1, 128, 4096])
        if epi == "C":
            wfl_d = din("w_fl", [D, 8])
        qT_o = dout("qT", [D, ntok], BF16); kT_o = dout("kT", [D, ntok], BF16)
        v_o = dout("v", [8, ntok, 256], BF16)
        if epi == "C":
            fl_o = dout("fl", [8, ntok])
    elif epi == "D":
        wmix_d = din("w_mix", [3 * D // 256, 1, 128, 4096])
        uT_o = dout("uT_o", [D, ntok]); bT_o = dout("bT_o", [D, ntok])
    if pro != "x":
        hout = dout("hout", [D, ntok])

    S = p.sb("S", [128, NCH, TT], F32); S_b = p.bufs(NCH, "S")
    hb = p.sb("hb", [128, NCH, TT], BF16); hb_b = p.bufs(NCH, "hb")
    u = p.sb("u", [128, 64, TT], BF16); u_b = p.bufs(64, "u")
    NSCR = 8
    scr = p.sb("scr", [128, NSCR, TT], F32); scr_b = p.bufs(NSCR, "scr")
    st = p.sb("st", [128, 4, TT], F32); st_b = p.bufs(4, "st")
    ones = p.sb("ones", [128, 128], BF16); ones_b = p.buf("ones")
    cst = p.sb("cst", [128, 8 * NCH], F32); cst_b = p.buf("cst")
    NPS = 6
    pst = [p.ps(f"ps{i}", [128, TT]) for i in range(NPS)]; ps_b = p.bufs(NPS, "ps")
    pstat = [p.ps(f"pstat{i}", [128, TT]) for i in range(2)]; pstat_b = p.bufs(2, "pstat")
    ws = WStream(p)
    ws.alloc()
    out_bufs = {}

    def obuf(name):
        if name not in out_bufs:
            out_bufs[name] = p.buf("o_" + name)
        return out_bufs[name]

    state = dict(scr=0, ps=0, ev=0)
    pend = []

    def flush():
        for f in pend:
            f()
        pend.clear()

    def nscr():
        i = state["scr"] % NSCR
        state["scr"] += 1
        return i

    def nps():
        i = state["ps"] % NPS
        state["ps"] += 1
        return i

    def scr_bf(i):
        return scr[:, i].bitcast(BF16)[:, 0:TT]

    p.dma("pool", ones[:], ones_d[:, :], reads=[], writes=[ones_b])
    if pro != "x":
        p.dma("sp", cst[:, 0:4 * NCH], lnp_d[:, :], reads=[], writes=[cst_b])
    if pro == "conv":
        p.dma("sp", cst[:, 4 * NCH:7 * NCH], cw_d[:, :], reads=[], writes=[cst_b])

    def lncol(which, o):
        return cst[:, which * NCH + o: which * NCH + o + 1]

    def mm_fm(W, col0, nout, nk, rhs, rhs_bufs, epilogue):
        nkb = nk // 16
        for cb in range(nout // 2):
            banks = [nps(), nps()]
            for kb in range(nkb):
                slot, sbuf = ws.next(W, kb * 16, col0 + cb * 256)
                if p.dry:
                    continue
                for oc in range(2):
                    for k in range(16):
                        kk = kb * 16 + k
                        p.mm(pst[banks[oc]][:], slot[:, k, oc * 128:(oc + 1) * 128], rhs(kk),
                             start=(kk == 0), stop=(kk == nk - 1),
                             reads=[sbuf, rhs_bufs[kk]], writes=[ps_b[banks[oc]]])
            if p.dry:
                continue
            flush()
            for oc in range(2):
                epilogue(cb * 2 + oc, pst[banks[oc]], ps_b[banks[oc]])
        flush()

    def mm_tm(W, col0, ncols, epilogue):
        for cb in range(ncols // 256):
            slot, sbuf = ws.next(W, 0, col0 + cb * 256)
            if p.dry:
                continue
            for tb in range(TT // 128):
                b = nps()
                for k in range(16):
                    p.mm(pst[b][:, 0:256], hb[:, k, tb * 128:(tb + 1) * 128], slot[:, k, :],
                         start=(k == 0), stop=(k == 15), reads=[sbuf, hb_b[k]], writes=[ps_b[b]])
                epilogue(cb, tb, pst[b], ps_b[b])

    def ln_stats_chunk(o, first, last):
        i1 = nscr(); i2 = nscr()
        p.act(scr_bf(i1), S[:, o], AF.Copy, reads=[S_b[o]], writes=[scr_b[i1]])
        p.act(scr_bf(i2), S[:, o], AF.Square, reads=[S_b[o]], writes=[scr_b[i2]])
        pend.append(lambda: p.mm(pstat[0][:], ones[:], scr_bf(i1), start=first, stop=last,
                                 reads=[ones_b, scr_b[i1]], writes=[pstat_b[0]]))
        pend.append(lambda: p.mm(pstat[1][:], ones[:], scr_bf(i2), start=first, stop=last,
                                 reads=[ones_b, scr_b[i2]], writes=[pstat_b[1]]))

    def ln_apply(gi, bi):
        V = "dve"
        p.op(V, lambda e: e.tensor_scalar(st[:, 0], pstat[0][:], 1.0 / D, None, ALU.mult),
             reads=[pstat_b[0]], writes=[st_b[0]])
        p.op(V, lambda e: e.tensor_tensor(st[:, 1], st[:, 0], st[:, 0], ALU.mult),
             reads=[st_b[0]], writes=[st_b[1]])
        p.op(V, lambda e: e.scalar_tensor_tensor(st[:, 2], pstat[1][:], 1.0 / D, st[:, 1], ALU.mult, ALU.subtract),
             reads=[pstat_b[1], st_b[1]], writes=[st_b[2]])
        p.op(V, lambda e: e.tensor_scalar(st[:, 2], st[:, 2], LN_EPS, None, ALU.add),
             reads=[st_b[2]], writes=[st_b[2]])
        p.act(st[:, 2], st[:, 2], AF.Sqrt, reads=[st_b[2]], writes=[st_b[2]])
        p.op(V, lambda e: e.reciprocal(st[:, 1], st[:, 2]), reads=[st_b[2]], writes=[st_b[1]])
        p.op(V, lambda e: e.scalar_tensor_tensor(st[:, 3], st[:, 0], -1.0, st[:, 1], ALU.mult, ALU.mult),
             reads=[st_b[0], st_b[1]], writes=[st_b[3]])
        for o in range(NCH):
            i1 = nscr()
            gcol = lncol(gi, o); bcol = lncol(bi, o)
            p.op(V, lambda e, o=o, i1=i1: e.tensor_tensor(scr[:, i1], S[:, o], st[:, 1], ALU.mult),
                 reads=[S_b[o], st_b[1]], writes=[scr_b[i1]])
            i2 = nscr()
            p.op(V, lambda e, i1=i1, i2=i2: e.tensor_tensor(scr[:, i2], scr[:, i1], st[:, 3], ALU.add),
                 reads=[scr_b[i1], st_b[3]], writes=[scr_b[i2]])
            p.act(S[:, o], scr[:, i2], AF.Identity, reads=[scr_b[i2], cst_b], writes=[S_b[o]],
                  scale=gcol, bias=bcol)
            p.act(hb[:, o], scr[:, i2], AF.Identity, reads=[scr_b[i2], cst_b], writes=[hb_b[o]],
                  scale=gcol, bias=bcol)

    def resid_epilogue(o, ps, psb):
        p.op("dve", lambda e: e.scalar_tensor_tensor(S[:, o], S[:, o], ALPHA, ps[:], ALU.mult, ALU.add),
             reads=[S_b[o], psb], writes=[S_b[o]])
        ln_stats_chunk(o, o == 0, o == NCH - 1)

    def relu2_epilogue(o, ps, psb):
        i1 = nscr()
        p.act(scr[:, i1], ps[:], AF.Relu, reads=[psb], writes=[scr_b[i1]])
        p.op("dve", lambda e: e.tensor_tensor(u[:, o], scr[:, i1], scr[:, i1], ALU.mult),
             reads=[scr_b[i1]], writes=[u_b[o]])

    def evac(dst, src, reads, writes, func=None, scale=None):
        state["ev"] += 1
        if func is not None or state["ev"] % 2 == 0:
            kw = {} if scale is None else dict(scale=scale)
            p.act(dst, src, func or AF.Copy, reads=reads, writes=writes, **kw)
        else:
            if scale is None:
                p.op("dve", lambda e: e.tensor_copy(dst, src), reads=reads, writes=writes)
            else:
                p.op("dve", lambda e: e.tensor_scalar(dst, src, scale, None, ALU.mult), reads=reads, writes=writes)

    def store_fm(dst_d, o, t, src_ap, src_buf, name):
        p.dma("sp", dst_d[o * 128:(o + 1) * 128, t * TT:(t + 1) * TT], src_ap, reads=[src_buf], writes=[obuf(name)])

    def record():
        ws.pos = 0
        for t in range(NT):
            tsl = slice(t * TT, (t + 1) * TT)
            for o in range(NCH):
                p.dma("sp", S[:, o], hin[o * 128:(o + 1) * 128, tsl], reads=[], writes=[S_b[o]])
            if pro == "x":
                for o in range(NCH):
                    evac(hb[:, o], S[:, o], [S_b[o]], [hb_b[o]])
            aT = u
            if pro == "attn":
                for o in range(NCH):
                    p.dma("sp", aT[:, o], oT_d[o * 128:(o + 1) * 128, tsl], reads=[], writes=[u_b[o]])
            if pro == "A2":
                for o in range(NCH):
                    i1 = nscr(); i2 = nscr()
                    p.dma("sp", scr_bf(i1), on_d[o * 128:(o + 1) * 128, tsl], reads=[], writes=[scr_b[i1]])
                    p.dma("sp", scr[:, i2], gs_d[o * 128:(o + 1) * 128, tsl], reads=[], writes=[scr_b[i2]])
                    p.op("dve", lambda e, o=o, i1=i1, i2=i2: e.tensor_tensor(aT[:, o], scr_bf(i1), scr[:, i2], ALU.mult),
                         reads=[scr_b[i1], scr_b[i2]], writes=[u_b[o]])
            if pro == "conv":
                for o in range(NCH):
                    i1 = nscr(); i2 = nscr(); i3 = nscr()
                    rows = slice(o * 128, (o + 1) * 128)
                    p.dma("sp", scr[:, i1], uT_d[rows, t * TT + 2:t * TT + 2 + TT], reads=[], writes=[scr_b[i1]])
                    p.dma("sp", scr[:, i2], uT_d[rows, t * TT + 1:t * TT + 1 + TT], reads=[], writes=[scr_b[i2]])
                    p.dma("sp", scr[:, i3], uT_d[rows, t * TT:t * TT + TT], reads=[], writes=[scr_b[i3]])
                    w2c = cst[:, 4 * NCH + 2 * NCH + o:4 * NCH + 2 * NCH + o + 1]
                    w1c = cst[:, 4 * NCH + 1 * NCH + o:4 * NCH + 1 * NCH + o + 1]
                    w0c = cst[:, 4 * NCH + 0 * NCH + o:4 * NCH + 0 * NCH + o + 1]
                    i4 = nscr()
                    p.op("dve", lambda e, i1=i1, i4=i4, w2c=w2c: e.tensor_scalar(scr[:, i4], scr[:, i1], w2c, None, ALU.mult),
                         reads=[scr_b[i1], cst_b], writes=[scr_b[i4]])
                    i5 = nscr()
                    p.op("dve", lambda e, i2=i2, i4=i4, i5=i5, w1c=w1c: e.scalar_tensor_tensor(scr[:, i5], scr[:, i2], w1c, scr[:, i4], ALU.mult, ALU.add),
                         reads=[scr_b[i2], scr_b[i4], cst_b], writes=[scr_b[i5]])
                    i6 = nscr()
                    p.op("dve", lambda e, i3=i3, i5=i5, i6=i6, w0c=w0c: e.scalar_tensor_tensor(scr[:, i6], scr[:, i3], w0c, scr[:, i5], ALU.mult, ALU.add),
                         reads=[scr_b[i3], scr_b[i5], cst_b], writes=[scr_b[i6]])
                    i7 = nscr()
                    p.dma("sp", scr[:, i7], bT_d[rows, tsl], reads=[], writes=[scr_b[i7]])
                    p.op("dve", lambda e, o=o, i6=i6, i7=i7: e.tensor_tensor(aT[:, o], scr[:, i6], scr[:, i7], ALU.mult),
                         reads=[scr_b[i6], scr_b[i7]], writes=[u_b[o]])
            if pro != "x":
                mm_fm(wout_d, 0, NCH, 16, lambda k: aT[:, k], u_b, resid_epilogue)
                if not p.dry:
                    ln_apply(0, 1)
                mm_fm(w1_d, 0, 64, 16, lambda k: hb[:, k], hb_b, relu2_epilogue)
                mm_fm(w2_d, 0, NCH, 64, lambda k: u[:, k], u_b, resid_epilogue)
                if not p.dry:
                    ln_apply(2, 3)
                    for o in range(NCH):
                        store_fm(hout, o, t, S[:, o], S_b[o], "hout")
            def fm_out(dst_d, name, dt_bf=False, func=None, scale=None):
                def ep(o, ps, psb):
                    i1 = nscr()
                    dst = scr_bf(i1) if dt_bf else scr[:, i1]
                    evac(dst, ps[:], [psb], [scr_b[i1]], func=func, scale=scale)
                    store_fm(dst_d, o, t, dst, scr_b[i1], name)
                return ep

            def tm_out(dst_d, name, dt_bf):
                def ep(cb, tb, ps, psb):
                    i1 = nscr()
                    dst = (scr_bf(i1) if dt_bf else scr[:, i1])[:, 0:256]
                    evac(dst, ps[:, 0:256], [psb], [scr_b[i1]])
                    p.dma("sp", dst_d[cb, t * TT + tb * 128:t * TT + (tb + 1) * 128, :], dst,
                          reads=[scr_b[i1]], writes=[obuf(name)])
                return ep

            if epi == "A":
                mm_fm(wmix_d, 0, NCH, 16, lambda k: hb[:, k], hb_b, fm_out(qT_o, "qT", dt_bf=True))
                mm_fm(wmix_d, D, NCH, 16, lambda k: hb[:, k], hb_b, fm_out(fT_o, "fT"))
                mm_fm(wmix_d, 3 * D, NCH, 16, lambda k: hb[:, k], hb_b, fm_out(gs_o, "gs", func=AF.Silu))
                mm_tm(wmix_d, 2 * D, D, tm_out(itok_o, "itok", True))
            elif epi in ("B", "C"):
                mm_fm(wmix_d, 0, NCH, 16, lambda k: hb[:, k], hb_b, fm_out(qT_o, "qT", dt_bf=True, scale=1.0 / 16.0))
                mm_fm(wmix_d, D, NCH, 16, lambda k: hb[:, k], hb_b, fm_out(kT_o, "kT", dt_bf=True))
                mm_tm(wmix_d, 2 * D, D, tm_out(v_o, "v", True))
                if epi == "C":
                    slot, sbuf = ws.next(wfl_d, 0, 0, 8)
                    if not p.dry:
                        b = nps()
                        for k in range(16):
                            p.mm(pst[b][0:8, :], slot[:, k, 0:8], hb[:, k], start=(k == 0), stop=(k == 15),
                                 reads=[sbuf, hb_b[k]], writes=[ps_b[b]])
                        i1 = nscr()
                        p.act(scr[0:8, i1], pst[b][0:8, :], AF.Copy, reads=[ps_b[b]], writes=[scr_b[i1]])
                        p.dma("sp", fl_o[:, tsl], scr[0:8, i1], reads=[scr_b[i1]], writes=[obuf("fl")])
            elif epi == "D":
                mm_fm(wmix_d, 0, NCH, 16, lambda k: hb[:, k], hb_b, fm_out(bT_o, "bT"))
                def cstage_v(o):
                    return u[:, 2 * o:2 * o + 2, :].bitcast(F32).rearrange("p a b -> p (a b)")
                def c_ep(o, ps, psb):
                    evac(cstage_v(o), ps[:], [psb], [u_b[2 * o], u_b[2 * o + 1]])
                mm_fm(wmix_d, D, NCH, 16, lambda k: hb[:, k], hb_b, c_ep)
                def h_ep(o, ps, psb):
                    i1 = nscr()
                    p.op("dve", lambda e: e.tensor_tensor(scr[:, i1], ps[:], cstage_v(o), ALU.mult),
                         reads=[psb, u_b[2 * o], u_b[2 * o + 1]], writes=[scr_b[i1]])
                    store_fm(uT_o, o, t, scr[:, i1], scr_b[i1], "uT")
                mm_fm(wmix_d, 2 * D, NCH, 16, lambda k: hb[:, k], hb_b, h_ep)
        if not p.dry:
            p.wait_all("sp", list(out_bufs.values()))

    p.dry = True
    record()
    p.dry = False
    state.update(scr=0, ps=0, ev=0)
    record()
    p.finish()
    return p
NEG = -30000.0


def build_attn(nc, kind, S=16384, io=None, tag=""):
    p = Prog(nc, tag)
    NB = S // 128
    NQT = S // 512
    fox = kind == "fox"

    def din(name, shape, dt=F32):
        if io is not None:
            ap = io[name]
            assert list(ap.shape) == list(shape) and ap.dtype == dt, (name, ap.shape, shape, ap.dtype, dt)
            return ap
        return nc.dram_tensor(name, list(shape), dt, kind="ExternalInput").ap()

    qT_d = din("qT", [256, S], BF16)
    kT_d = din("kT", [256, S], BF16)
    v_d = din("v", [S, 256], BF16)
    mask_d = din("mask", [128, 128])
    ident_d = din("ident", [128, 128])
    if io is not None:
        o_d = io["oT"]
        assert list(o_d.shape) == [256, S]
    else:
        o_d = nc.dram_tensor("oT", [256, S], BF16, kind="ExternalOutput").ap()
    if fox:
        fl_d = din("fl", [NB, 128])
        bf_d = din("bf", [128, 1])
        ltri_d = din("ltri", [128, 128])
        augk_d = din("augk", [128, 128])
        caug_d = nc.dram_tensor(tag + "caug_scr", [3, S], BF16, kind="Internal").ap()
    else:
        utri_d = din("utri", [128, 128])
        nones_d = din("nones", [128, 128])

    VW = 257 if fox else 256
    kT = p.sb("kT", [128, 2, S], BF16); kT_b = p.buf("kT")
    v = p.sb("v", [128, NB, VW], BF16); v_b = p.buf("v")
    NQ = 2
    qt = p.sb("qt", [128, NQ, 2, 512], BF16); qt_b = p.bufs(NQ, "qt")
    mask = p.sb("mask", [128, 128], BF16); ident = p.sb("ident", [128, 128], BF16); c_b = p.buf("consts")
    NP = 3
    pT = p.sb("pT", [128, NP, 512], BF16); pT_b = p.bufs(NP, "pT")
    NO = 4
    ost = p.sb("ost", [128, NO, 256], BF16); ost_b = p.bufs(NO, "ost")
    rc = p.sb("rc", [128, NO], F32); rc_b = p.bufs(NO, "rc")
    NSC = 3
    sc = [p.ps(f"sc{i}", [128, 512]) for i in range(NSC)]; sc_b = p.bufs(NSC, "sc")
    oacc = [p.ps(f"oacc{i}", [128, 512]) for i in range(4)]; oacc_b = p.bufs(4, "oacc")
    out_b = p.buf("out")
    trp = p.ps("trp", [128, 2, 128], BF16); trp_b = p.buf("trp")
    NOT = 3
    oTs = p.sb("oTs", [128, NOT, 2, 128], BF16); oTs_b = p.bufs(NOT, "oTs")
    tr_pend = []

    def flush_tr():
        for f in tr_pend:
            f()
        tr_pend.clear()

    p.dma("pool", mask[:], mask_d[:, :], reads=[], writes=[c_b])
    p.dma("pool", ident[:], ident_d[:, :], reads=[], writes=[c_b])
    for c in range(2):
        for g in range(4):
            sl = slice(g * S // 4, (g + 1) * S // 4)
            p.dma("sp", kT[:, c, sl], kT_d[c * 128:(c + 1) * 128, sl], reads=[], writes=[kT_b])
    vv = v_d.rearrange("(b s) d -> s b d", s=128)
    for g in range(4):
        bs = slice(g * NB // 4, (g + 1) * NB // 4)
        p.dma("sp", v[:, bs, 0:256], vv[:, bs, :], reads=[], writes=[v_b])

    if fox:
        p.op("dve", lambda e: e.memset(v[:, :, 256:257], 1.0), reads=[], writes=[v_b])
        ltri = p.sb("ltri", [128, 128], F32); identf = p.sb("identf", [128, 128], F32)
        augk = p.sb("augk", [128, 128], BF16)
        p.dma("sp", ltri[:], ltri_d[:, :], reads=[], writes=[c_b])
        p.dma("sp", identf[:], ident_d[:, :], reads=[], writes=[c_b])
        p.dma("pool", augk[:], augk_d[:, :], reads=[], writes=[c_b])
        bfc = p.sb("bfc", [128, 2], F32); bfc_b = p.buf("bfc")
        p.dma("sp", bfc[:, 0:1], bf_d[:, :], reads=[], writes=[bfc_b])
        fa = p.sb("fa", [128, 2, 128], F32); fa_b = p.bufs(2, "fa")
        fb = p.sb("fb", [128, 128], F32); fb_b = p.buf("fb")
        cn = p.sb("cn", [128, 128], F32); cn_b = p.buf("cn")
        cnT = p.sb("cnT", [128, 128], F32); cnT_b = p.buf("cnT")
        pc = p.sb("pc", [128, 3, 128], BF16); pc_b = p.bufs(3, "pc")
        rr = p.sb("rr", [128, 2, 128], F32); rr_b = p.bufs(2, "rr")
        qaug = p.sb("qaug", [128, NQ, 512], BF16); qaug_b = p.bufs(NQ, "qaug")
        nb_ = NB
        p.dma("sp", fa[0:nb_, 0], fl_d[:, :], reads=[], writes=[fa_b[0]])
        p.op("dve", lambda e: e.tensor_scalar(bfc[:, 1:2], bfc[:, 0:1], -1.0, None, ALU.mult), reads=[bfc_b], writes=[bfc_b])
        p.act(fa[0:nb_, 1], fa[0:nb_, 0], AF.Exp, reads=[fa_b[0], bfc_b], writes=[fa_b[1]], scale=-1.0, bias=bfc[0:nb_, 1:2])
        p.act(fa[0:nb_, 0], fa[0:nb_, 1], AF.Ln, reads=[fa_b[1]], writes=[fa_b[0]], bias=1.0)
        cur = 0
        d = 1
        while d < 128:
            nxt = 1 - cur
            p.op("dve", lambda e, cur=cur, nxt=nxt, d=d: e.tensor_tensor(fa[0:nb_, nxt, d:128], fa[0:nb_, cur, d:128], fa[0:nb_, cur, 0:128 - d], ALU.add),
                 reads=[fa_b[cur]], writes=[fa_b[nxt]])
            p.op("dve", lambda e, cur=cur, nxt=nxt, d=d: e.tensor_copy(fa[0:nb_, nxt, 0:d], fa[0:nb_, cur, 0:d]),
                 reads=[fa_b[cur]], writes=[fa_b[nxt]])
            cur = nxt
            d *= 2
        p.mm(sc[0][0:nb_, 0:1], ltri[0:nb_, 0:nb_], fa[0:nb_, cur, 127:128], True, True,
             reads=[c_b, fa_b[cur]], writes=[sc_b[0]])
        p.op("dve", lambda e: e.tensor_copy(fb[0:nb_, 0:1], sc[0][0:nb_, 0:1]), reads=[sc_b[0]], writes=[fb_b])
        if nb_ < 128:
            p.op("dve", lambda e: e.memset(cn[:], 0.0), reads=[], writes=[cn_b])
        p.op("dve", lambda e, cur=cur: e.tensor_scalar(cn[0:nb_, :], fa[0:nb_, cur, :], fb[0:nb_, 0:1], None, ALU.add),
             reads=[fa_b[cur], fb_b], writes=[cn_b])
        p.mm(sc[1][:, 0:128], cn[:], identf[:], True, True, reads=[cn_b, c_b], writes=[sc_b[1]])
        p.op("dve", lambda e: e.tensor_copy(cnT[:], sc[1][:, 0:128]), reads=[sc_b[1]], writes=[cnT_b])
        p.op("dve", lambda e: e.tensor_scalar(rr[:, 0], cn[:], -1.0, None, ALU.mult), reads=[cn_b], writes=[rr_b[0]])
        p.act(pc[:, 0], rr[:, 0], AF.Copy, reads=[rr_b[0]], writes=[pc_b[0]])
        p.op("dve", lambda e: e.tensor_tensor(rr[:, 1], rr[:, 0], pc[:, 0], ALU.subtract), reads=[rr_b[0], pc_b[0]], writes=[rr_b[1]])
        p.act(pc[:, 1], rr[:, 1], AF.Copy, reads=[rr_b[1]], writes=[pc_b[1]])
        p.op("dve", lambda e: e.tensor_tensor(rr[:, 0], rr[:, 1], pc[:, 1], ALU.subtract), reads=[rr_b[1], pc_b[1], rr_b[0]], writes=[rr_b[0]])
        p.act(pc[:, 2], rr[:, 0], AF.Copy, reads=[rr_b[0]], writes=[pc_b[2]])
        caug_b = p.buf("caug")
        for i in range(3):
            p.dma("sp", caug_d[i:i + 1, :].rearrange("o (b s) -> (o b) s", s=128), pc[0:nb_, i], reads=[pc_b[i]], writes=[caug_b])
        for i in range(NQ):
            p.op("dve", lambda e, i=i: e.memset(qaug[:, i], 0.0), reads=[], writes=[qaug_b[i]])
    else:
        utri = p.sb("utri", [128, 128], BF16); nones = p.sb("nones", [128, 128], BF16)
        p.dma("pool", utri[:], utri_d[:, :], reads=[], writes=[c_b])
        p.dma("pool", nones[:], nones_d[:, :], reads=[], writes=[c_b])
        NE = 2
        e1 = p.sb("e1", [128, NE, 512], F32); e1_b = p.bufs(NE, "e1")
        NS = 4
        sp = p.sb("sp", [128, NS, 512], BF16); sp_b = p.bufs(NS, "sp")
        R32 = p.sb("R32", [128, 512], F32); R32_b = p.buf("R32")
        NR = 3
        Rb = p.sb("Rb", [128, NR, 512], BF16); Rb_b = p.bufs(NR, "Rb")

    cnt = dict(sc=0, pT=0, o=0, sp=0, e1=0, rb=0, ot=0)

    def rot(key, n):
        i = cnt[key] % n
        cnt[key] += 1
        return i

    def finalize(i, ob):
        oi = rot("o", NO)
        if fox:
            p.op("dve", lambda e: e.reciprocal(rc[:, oi:oi + 1], oacc[ob][:, 256:257]), reads=[oacc_b[ob]], writes=[rc_b[oi]])
            p.act(ost[:, oi], oacc[ob][:, 0:256], AF.Copy, reads=[oacc_b[ob], rc_b[oi]], writes=[ost_b[oi]], scale=rc[:, oi:oi + 1])
        else:
            p.act(ost[:, oi], oacc[ob][:, 0:256], AF.Copy, reads=[oacc_b[ob]], writes=[ost_b[oi]])
        def tr(i=i, oi=oi):
            ti = rot("ot", NOT)
            for c in range(2):
                p.op("pe", lambda e, c=c: e.transpose(trp[:, c], ost[:, oi, c * 128:(c + 1) * 128], ident[:]),
                     reads=[ost_b[oi], c_b], writes=[trp_b])
            p.op("dve", lambda e: e.tensor_copy(oTs[:, ti], trp[:]), reads=[trp_b], writes=[oTs_b[ti]])
            for c in range(2):
                p.dma("sp", o_d[c * 128:(c + 1) * 128, i * 128:(i + 1) * 128], oTs[:, ti, c], reads=[oTs_b[ti]], writes=[out_b])
        tr_pend.append(tr)

    def pv(T, j, pi, c0):
        for i in range(max(j, 4 * T), 4 * T + 4):
            cols = (i - 4 * T) * 128
            ob = i % 4
            if fox:
                st_, sp_ = (j == 0), (j == i)
            else:
                st_, sp_ = (j == i), (j == 0)
            p.mm(oacc[ob][:, 0:VW], pT[:, pi, cols:cols + 128], v[:, j, :], st_, sp_,
                 reads=[pT_b[pi], v_b], writes=[oacc_b[ob]])
            if sp_:
                finalize(i, ob)

    for T in range(NQT):
        qi = T % NQ
        for c in range(2):
            p.dma("sp", qt[:, qi, c, :], qT_d[c * 128:(c + 1) * 128, T * 512:(T + 1) * 512], reads=[], writes=[qt_b[qi]])
        if fox:
            for i in range(3):
                p.dma("sp", qaug[32 * i:32 * i + 1, qi, :], caug_d[i:i + 1, T * 512:(T + 1) * 512], reads=[caug_b], writes=[qaug_b[qi]])
        nj = 4 * T + 4
        order = list(range(nj)) if fox else list(range(nj - 1, -1, -1))
        pend = []
        if not fox:
            first = [True]
        for idx, j in enumerate(order):
            c0 = max(0, (j - 4 * T)) * 128
            si = rot("sc", NSC)
            w = 512 - c0
            diag = j >= 4 * T
            p.mm(sc[si][:, c0:512], kT[:, 0, j * 128:(j + 1) * 128], qt[:, qi, 0, c0:512], True, False,
                 reads=[kT_b, qt_b[qi]], writes=[sc_b[si]])
            last_qk = False
            p.mm(sc[si][:, c0:512], kT[:, 1, j * 128:(j + 1) * 128], qt[:, qi, 1, c0:512], False, last_qk,
                 reads=[kT_b, qt_b[qi]], writes=[sc_b[si]])
            if fox:
                p.mm(sc[si][:, c0:512], augk[:], qaug[:, qi, c0:512], False, not diag,
                     reads=[c_b, qaug_b[qi]], writes=[sc_b[si]])
            if diag:
                p.mm(sc[si][:, c0:c0 + 128], ident[:], mask[:], False, fox, reads=[c_b], writes=[sc_b[si]])
            flush_tr()
            if fox:
                pi = rot("pT", NP)
                p.act(pT[:, pi, c0:512], sc[si][:, c0:512], AF.Exp, reads=[sc_b[si], cnT_b], writes=[pT_b[pi]],
                      bias=cnT[:, j:j + 1])
                for f in pend:
                    f()
                pend = [lambda T=T, j=j, pi=pi, c0=c0: pv(T, j, pi, c0)]
            else:
                ei = rot("e1", NE); spi = rot("sp", NS)
                p.act(e1[:, ei, c0:512], sc[si][:, c0:512], AF.Exp, reads=[sc_b[si]], writes=[e1_b[ei]])
                p.act(sp[:, spi, c0:512], e1[:, ei, c0:512], AF.Ln, reads=[e1_b[ei]], writes=[sp_b[spi]], bias=1.0)
                ri = None
                if idx > 0:
                    ri = (cnt["rb"] - 1) % NR
                def stageB(T=T, j=j, si=si, spi=spi, c0=c0, ri=ri, idx=idx):
                    p.mm(sc[si][:, c0:512], utri[:], sp[:, spi, c0:512], False, idx == 0,
                         reads=[c_b, sp_b[spi]], writes=[sc_b[si]])
                    if idx > 0:
                        p.mm(sc[si][:, c0:512], nones[:], Rb[:, ri, c0:512], False, True,
                             reads=[c_b, Rb_b[ri]], writes=[sc_b[si]])
                    pi = rot("pT", NP)
                    p.act(pT[:, pi, c0:512], sc[si][:, c0:512], AF.Exp, reads=[sc_b[si]], writes=[pT_b[pi]])
                    return lambda: pv(T, j, pi, c0)
                if idx == 0:
                    if c0 > 0:
                        p.op("dve", lambda e, c0=c0: e.memset(R32[:, 0:c0], 0.0), reads=[], writes=[R32_b])
                    p.op("dve", lambda e, spi=spi, c0=c0: e.tensor_copy(R32[:, c0:512], sp[:, spi, c0:512]), reads=[sp_b[spi]], writes=[R32_b])
                else:
                    p.op("dve", lambda e, spi=spi, c0=c0: e.tensor_tensor(R32[:, c0:512], R32[:, c0:512], sp[:, spi, c0:512], ALU.add),
                         reads=[sp_b[spi], R32_b], writes=[R32_b])
                rn = rot("rb", NR)
                p.op("dve", lambda e, rn=rn: e.tensor_copy(Rb[:, rn], R32[:]), reads=[R32_b], writes=[Rb_b[rn]])
                newp = []
                for f in pend:
                    r = f()
                    if r is not None:
                        newp.append(r)
                pend = newp + [stageB]
        while pend:
            newp = []
            for f in pend:
                r = f()
                if r is not None:
                    newp.append(r)
            pend = newp
    flush_tr()
    p.wait_all("sp", [out_b])
    p.finish()
    return p


def build_hgrn(nc, S=16384, io=None, tag=""):
    p = Prog(nc, tag)
    NT = S // 512
    NB = S // 128

    def din(name, shape, dt=F32):
        if io is not None:
            ap = io[name]
            assert list(ap.shape) == list(shape) and ap.dtype == dt, (name, ap.shape, shape, ap.dtype, dt)
            return ap
        return nc.dram_tensor(name, list(shape), dt, kind="ExternalInput").ap()

    zq_d = din("zqT", [256, S], BF16); zf_d = din("zfT", [256, S])
    v_d = din("vtok", [S, 256], BF16)
    lbl_d = din("lbl", [128, 10])
    ng_d = din("ng", [128, 1])
    ident_d = din("ident", [128, 128]); ones_d = din("ones", [128, 128])
    cmask_d = din("cmask", [128, 128]); rmask_d = din("rmask", [128, 4])
    if io is not None:
        on_d = io["on"]
        assert list(on_d.shape) == [256, S] and on_d.dtype == BF16
    else:
        on_d = nc.dram_tensor("on", [256, S], BF16, kind="ExternalOutput").ap()

    c_b = p.buf("consts")
    ident = p.sb("ident", [128, 128], BF16); ones = p.sb("ones", [128, 128], BF16)
    cmask = p.sb("cmask", [128, 128], F32); rmask = p.sb("rmask", [128, 4], F32)
    ng = p.sb("ng", [128, 1], F32)
    lbl = p.sb("lbl", [128, 2, 5], F32); lbe = p.sb("lbe", [128, 2, 5], F32); lbs = p.sb("lbs", [128, 8], F32)
    lb_b = p.buf("lb")
    p.dma("pool", ident[:], ident_d[:, :], reads=[], writes=[c_b])
    p.dma("pool", ones[:], ones_d[:, :], reads=[], writes=[c_b])
    p.dma("sp", cmask[:], cmask_d[:, :], reads=[], writes=[c_b])
    p.dma("sp", rmask[:], rmask_d[:, :], reads=[], writes=[c_b])
    p.dma("sp", ng[:], ng_d[:, :], reads=[], writes=[c_b])
    p.dma("sp", lbl[:], lbl_d.rearrange("p (h r) -> p h r", r=5), reads=[], writes=[lb_b])
    p.act(lbe[:], lbl[:], AF.Exp, reads=[lb_b], writes=[lb_b])
    p.op("dve", lambda e: e.reduce_sum(lbs[:, 0:2], lbe[:], AX.X), reads=[lb_b], writes=[lb_b])
    p.op("dve", lambda e: e.reciprocal(lbs[:, 2:4], lbs[:, 0:2]), reads=[lb_b], writes=[lb_b])
    p.op("dve", lambda e: e.tensor_tensor(lbs[:, 4:6], lbe[:, :, 0], lbs[:, 2:4], ALU.mult), reads=[lb_b], writes=[lb_b])
    p.op("dve", lambda e: e.tensor_scalar(lbs[:, 6:8], lbs[:, 4:6], -1.0, 1.0, ALU.mult, ALU.add), reads=[lb_b], writes=[lb_b])

    v = p.sb("v", [128, NB, 256], BF16); v_b = p.buf("v")
    vv = v_d.rearrange("(b s) d -> s b d", s=128)
    for g in range(4):
        bs = slice(g * NB // 4, (g + 1) * NB // 4)
        p.dma("sp", v[:, bs, :], vv[:, bs, :], reads=[], writes=[v_b])

    NF = 10
    ft = [p.sb(f"ft{h}", [128, NF, 512], F32) for h in range(2)]
    ft_b = [p.bufs(NF, f"ft{h}_") for h in range(2)]
    bt = [p.sb(f"bt{h}", [128, 3, 512], BF16) for h in range(2)]
    bt_b = [p.bufs(3, f"bt{h}_") for h in range(2)]
    S32 = [p.sb(f"S32_{h}", [128, 128], F32) for h in range(2)]; S32_b = p.bufs(2, "S32")
    Sb = [p.sb(f"Sb_{h}", [128, 2, 128], BF16) for h in range(2)]; Sb_b = [p.bufs(2, f"Sb{h}_") for h in range(2)]
    scm = p.sb("scm", [128, 2, 128], BF16); scm_b = p.bufs(2, "scm")
    kdm = p.sb("kdm", [128, 2, 4, 128], BF16); kdm_b = p.bufs(2, "kdm")
    o32 = p.sb("o32", [128, 2, 128], F32); o32_b = p.bufs(2, "o32")
    osq = p.sb("osq", [128, 2, 128], BF16); osq_b = p.bufs(2, "osq")
    rt = p.sb("rt", [128, 2, 128], F32); rt_b = p.bufs(2, "rt")
    onst = p.sb("onst", [128, 2, 512], BF16); onst_b = p.bufs(2, "onst")
    zqb = p.sb("zqb", [128, 2, 512], BF16); zqb_b = p.bufs(2, "zqb")
    pA = [p.ps(f"pA{i}", [128, 128]) for i in range(2)]; pA_b = p.bufs(2, "pA")
    pT = p.ps("pT", [128, 128], BF16); pT_b = p.buf("pT")
    pO = [p.ps(f"pO{i}", [128, 128]) for i in range(2)]; pO_b = p.bufs(2, "pO")
    pD = [p.ps(f"pD{i}", [128, 128]) for i in range(2)]; pD_b = p.bufs(2, "pD")
    pM = p.ps("pM", [128, 128]); pM_b = p.buf("pM")
    out_b = p.buf("out")
    cnt = dict(a=0, d=0, o=0, x=0)

    def rot(k, n):
        i = cnt[k] % n
        cnt[k] += 1
        return i

    for h in range(2):
        p.op("dve", lambda e, h=h: e.memset(S32[h][:], 0.0), reads=[], writes=[S32_b[h]])
        p.op("dve", lambda e, h=h: e.memset(Sb[h][:], 0.0), reads=[], writes=[Sb_b[h][0], Sb_b[h][1]])
    sbi = [0, 0]
    V = "dve"
    ZQ, ZF, FF, KK, SA, SBB, EBC, ENB, QS, KI = range(10)
    for t in range(NT):
        tsl = slice(t * 512, (t + 1) * 512)
        for h in range(2):
            F = ft[h]; Fb = ft_b[h]; B = bt[h]; Bb = bt_b[h]
            lbc = lbs[:, 4 + h:5 + h]; omlc = lbs[:, 6 + h:7 + h]
            p.dma("sp", zqb[:, h], zq_d[h * 128:(h + 1) * 128, tsl], reads=[], writes=[zqb_b[h]])
            p.dma("sp", F[:, ZF], zf_d[h * 128:(h + 1) * 128, tsl], reads=[], writes=[Fb[ZF]])
            p.act(F[:, FF], F[:, ZF], AF.Sigmoid, reads=[Fb[ZF]], writes=[Fb[FF]])
            p.op(V, lambda e, F=F, lbc=lbc, omlc=omlc: e.tensor_scalar(F[:, FF], F[:, FF], omlc, lbc, ALU.mult, ALU.add),
                 reads=[Fb[FF], lb_b], writes=[Fb[FF]])
            p.act(F[:, SA], F[:, FF], AF.Ln, reads=[Fb[FF]], writes=[Fb[SA]])
            p.op(V, lambda e, F=F: e.tensor_scalar(F[:, KK], F[:, FF], -1.0, 1.0, ALU.mult, ALU.add), reads=[Fb[FF]], writes=[Fb[KK]])
            cur, nxt = SA, SBB
            d = 1
            while d < 16:
                xv = F[:, cur].rearrange("p (c j) -> p c j", j=16)
                yv = F[:, nxt].rearrange("p (c j) -> p c j", j=16)
                p.op(V, lambda e, xv=xv, yv=yv, d=d: e.tensor_tensor(yv[:, :, d:16], xv[:, :, d:16], xv[:, :, 0:16 - d], ALU.add),
                     reads=[Fb[cur]], writes=[Fb[nxt]])
                p.op(V, lambda e, xv=xv, yv=yv, d=d: e.tensor_copy(yv[:, :, 0:d], xv[:, :, 0:d]), reads=[Fb[cur]], writes=[Fb[nxt]])
                cur, nxt = nxt, cur
                d *= 2
            BC = cur
            p.act(F[:, EBC], F[:, BC], AF.Exp, reads=[Fb[BC]], writes=[Fb[EBC]])
            p.act(F[:, ENB], F[:, BC], AF.Exp, reads=[Fb[BC]], writes=[Fb[ENB]], scale=-1.0)
            p.act(F[:, QS], zqb[:, h], AF.Silu, reads=[zqb_b[h]], writes=[Fb[QS]])
            p.op(V, lambda e, F=F, B=B: e.scalar_tensor_tensor(B[:, 0], F[:, QS], 128.0 ** -0.5, F[:, EBC], ALU.mult, ALU.mult),
                 reads=[Fb[QS], Fb[EBC]], writes=[Bb[0]])
            p.op(V, lambda e, F=F: e.tensor_tensor(F[:, KI], F[:, KK], F[:, ENB], ALU.mult), reads=[Fb[KK], Fb[ENB]], writes=[Fb[KI]])
            p.act(B[:, 1], F[:, KI], AF.Copy, reads=[Fb[KI]], writes=[Bb[1]])
            dlv = F[:, EBC].rearrange("p (c j) -> p c j", j=16)[:, :, 15:16].to_broadcast([128, 32, 16])
            p.op(V, lambda e, F=F, B=B, dlv=dlv: e.tensor_tensor(B[:, 2].rearrange("p (c j) -> p c j", j=16),
                                                            F[:, KI].rearrange("p (c j) -> p c j", j=16), dlv, ALU.mult),
                 reads=[Fb[KI], Fb[EBC]], writes=[Bb[2]])
        for b in range(4):
            blk = t * 4 + b
            bs = slice(b * 128, (b + 1) * 128)
            for h in range(2):
                F = ft[h]; Fb = ft_b[h]; B = bt[h]; Bb = bt_b[h]
                vh = slice(h * 128, (h + 1) * 128)
                ai = h
                p.mm(pA[ai][:], B[:, 1, bs], B[:, 0, bs], True, True, reads=[Bb[1], Bb[0]], writes=[pA_b[ai]])
                p.op(V, lambda e, ai=ai: e.tensor_tensor(scm[:, ai], pA[ai][:], cmask[:], ALU.mult), reads=[pA_b[ai], c_b], writes=[scm_b[ai]])
                p.op("pe", lambda e, B=B, bs=bs: e.transpose(pT[:], B[:, 2, bs], ident[:]), reads=[Bb[2], c_b], writes=[pT_b])
                for q in range(4):
                    if q % 2 == 0:
                        p.act(kdm[:, ai, q], pT[:], AF.Copy, reads=[pT_b, c_b], writes=[kdm_b[ai]], scale=rmask[:, q:q + 1])
                    else:
                        p.op(V, lambda e, ai=ai, q=q: e.tensor_scalar(kdm[:, ai, q], pT[:], rmask[:, q:q + 1], None, ALU.mult),
                             reads=[pT_b, c_b], writes=[kdm_b[ai]])
                p.mm(pO[h][:], v[:, blk, vh], scm[:, ai], True, False, reads=[v_b, scm_b[ai]], writes=[pO_b[h]])
            for n in range(8):
                for h in range(2):
                    F = ft[h]; Fb = ft_b[h]; B = bt[h]; Bb = bt_b[h]
                    vh = slice(h * 128, (h + 1) * 128)
                    ai = h
                    col = b * 128 + n * 16
                    si = sbi[h]
                    p.mm(pO[h][:, n * 16:(n + 1) * 16], Sb[h][:, si], B[:, 0, col:col + 16], False, n == 7,
                         reads=[Sb_b[h][si], Bb[0]], writes=[pO_b[h]])
                    half = slice(64 * (n // 4), 64 * (n // 4) + 64)
                    p.mm(pD[h][:], kdm[half, ai, n % 4], v[half, blk, vh], True, True, reads=[kdm_b[ai], v_b], writes=[pD_b[h]])
                    p.op(V, lambda e, h=h, F=F, col=col: e.scalar_tensor_tensor(S32[h][:], S32[h][:], F[:, EBC, col + 15:col + 16], pD[h][:], ALU.mult, ALU.add),
                         reads=[S32_b[h], Fb[EBC], pD_b[h]], writes=[S32_b[h]])
                    sn = 1 - si
                    p.act(Sb[h][:, sn], S32[h][:], AF.Copy, reads=[S32_b[h]], writes=[Sb_b[h][sn]])
                    sbi[h] = sn
            for h in range(2):
                ai = h; oi = h
                p.act(o32[:, ai], pO[h][:], AF.Copy, reads=[pO_b[h]], writes=[o32_b[ai]])
                p.act(osq[:, ai], pO[h][:], AF.Square, reads=[pO_b[h]], writes=[osq_b[ai]])
                p.mm(pM[:], ones[:], osq[:, ai], True, True, reads=[c_b, osq_b[ai]], writes=[pM_b])
                p.op(V, lambda e, ai=ai: e.tensor_scalar(rt[:, ai], pM[:], 1.0 / 128.0, 1e-6, ALU.mult, ALU.add), reads=[pM_b], writes=[rt_b[ai]])
                p.act(rt[:, ai], rt[:, ai], AF.Sqrt, reads=[rt_b[ai]], writes=[rt_b[ai]])
                p.op(V, lambda e, ai=ai: e.reciprocal(rt[:, ai], rt[:, ai]), reads=[rt_b[ai]], writes=[rt_b[ai]])
                p.op(V, lambda e, ai=ai, oi=oi, bs=bs: e.scalar_tensor_tensor(onst[:, oi, bs], o32[:, ai], ng[:, 0:1], rt[:, ai], ALU.mult, ALU.mult),
                     reads=[o32_b[ai], rt_b[ai], c_b], writes=[onst_b[oi]])
        for h in range(2):
            p.dma("sp", on_d[h * 128:(h + 1) * 128, tsl], onst[:, h], reads=[onst_b[h]], writes=[out_b])
    p.wait_all("sp", [out_b])
    p.finish()
    return p
import ml_dtypes as _mld

_BF = _mld.bfloat16
NCORE = 8
SEQ = 16384
TOK = SEQ // NCORE


def _zero_block(nc, E3, bufs):
    dsem, dbase = sem_alloc(nc, "zi_d")
    msem, mbase = sem_alloc(nc, "zi_m")
    fin = {}
    with nc.sbuf_tensor("zt", [128, 8192], F32) as zt, nc.Block() as blk:
        def snaps(e, key):
            pid = e.partition_id()
            E3[key] = dict(tok=e.snap(pid * TOK), h=e.snap(pid * 256), p=e.snap(pid), p8=e.snap(pid * 8), p1=e.snap(pid + 1))

        @blk.gpsimd
        def _(g):
            snaps(g, "pool")
            g.memset(zt[:], 0.0).then_inc(msem, 1)
            g.wait_ge(msem, mbase + 1)
            n = dbase
            ztb = zt[:].bitcast(BF16)
            GA_in, GB_in, GF_in, GH_in = bufs
            for a in range(NCORE * TOK // 128 // 8):
                dst = GA_in.ap()[a * 1024:(a + 1) * 1024, :].rearrange("(a p) w -> p a w", p=128)
                g.dma_start(out=dst, in_=ztb.rearrange("p (a w) -> p a w", a=8)).then_inc(dsem, 16); n += 16
            for a in range(NCORE * TOK // 128 // 4):
                dst = GB_in.ap()[a * 512:(a + 1) * 512, :].rearrange("(a p) w -> p a w", p=128)
                g.dma_start(out=dst, in_=zt[:].rearrange("p (a w) -> p a w", a=4)).then_inc(dsem, 16); n += 16
            g.dma_start(out=GF_in.ap()[:, :], in_=zt[0:64, 0:TOK]).then_inc(dsem, 16); n += 16
            g.dma_start(out=GH_in.ap()[:, :], in_=zt[0:NCORE + 1, 0:2 * D]).then_inc(dsem, 16); n += 16
            g.wait_ge(dsem, n)
            fin["d"] = n

        @blk.sync
        def _(e):
            snaps(e, "sp")

        @blk.scalar
        def _(e):
            snaps(e, "act")
    sem_free(nc, dsem, fin["d"])
    sem_free(nc, msem, mbase + 1)


_RR = dict(i=0)
XENGS = ("sp", "act", "pool")


def _xblock(nc, name, E3, transfers):
    engs = XENGS
    plan = []
    for (slot_in, Gi, Go, outs) in transfers:
        ea = engs[_RR["i"] % len(engs)]; _RR["i"] += 1
        eo = []
        for _ in outs:
            eo.append(engs[_RR["i"] % len(engs)]); _RR["i"] += 1
        plan.append((slot_in, ea, Gi, Go, outs, eo))
    isem, ib = sem_alloc(nc, name + "_i")
    csem, cb = sem_alloc(nc, name + "_c")
    osem, ob = sem_alloc(nc, name + "_o")
    fin = {}
    with nc.Block() as blk:
        def body(engname):
            def f(e):
                ic = ib; cc = cb; oc = ob
                for (slot_in, ea, Gi, Go, outs, eo) in plan:
                    if engname == ea:
                        e.wait_ge(osem, oc)
                        o_, i_ = slot_in(E3[engname])
                        e.dma_start(out=o_, in_=i_).then_inc(isem, 16)
                    ic += 16
                    if engname == "pool":
                        e.wait_ge(isem, ic)
                        e.collective_compute("AllReduce", ALU.add, replica_groups=[list(range(NCORE))],
                                             ins=[Gi.ap().opt()], outs=[Go.ap().opt()]).then_inc(csem)
                    cc += 1
                    for fn_, en_ in zip(outs, eo):
                        if engname == en_:
                            e.wait_ge(csem, cc)
                            o_, i_ = fn_(E3[engname])
                            e.dma_start(out=o_, in_=i_).then_inc(osem, 16)
                        oc += 16
                e.wait_ge(osem, oc)
                fin["v"] = (ic, cc, oc)
            return f
        blk.gpsimd(body("pool"))
        blk.sync(body("sp"))
        blk.scalar(body("act"))
    sem_free(nc, isem, fin["v"][0])
    sem_free(nc, csem, fin["v"][1])
    sem_free(nc, osem, fin["v"][2])


def build_fused(nc):
    S = SEQ

    def ext(name, shape, dt=F32):
        return nc.dram_tensor(name, list(shape), dt, kind="ExternalInput").ap()

    def itn(name, shape, dt=F32):
        return nc.dram_tensor(name, list(shape), dt)

    E = {}
    E["xT"] = ext("xT", [D, TOK])
    for nm, ncol in (("wmix_a", 4 * D), ("wmix_b", 3 * D), ("wmix_c", 3 * D), ("wmix_d", 3 * D)):
        E[nm] = ext(nm, [ncol // 256, 1, 128, 4096])
    E["w_fl"] = ext("w_fl", [D, 8])
    for nm in ("wout_a", "wout_b", "wout_c", "wout_d"):
        E[nm] = ext(nm, [D // 256, 1, 128, 4096])
    for i in range(4):
        E[f"w1_{i}"] = ext(f"w1_{i}", [DFF // 256, 1, 128, 4096])
        E[f"w2_{i}"] = ext(f"w2_{i}", [D // 256, 4, 128, 4096])
        E[f"lnp_{i}"] = ext(f"lnp_{i}", [128, 4 * NCH])
    E["cw"] = ext("cw", [128, 3 * NCH])
    for nm in ("ones", "ident", "mask_sb", "mask_fx", "utri", "nones", "ltri", "augk", "cmask"):
        E[nm] = ext(nm, [128, 128])
    E["rmask"] = ext("rmask", [128, 4])
    E["lbl"] = ext("lbl", [128, 10])
    E["ng"] = ext("ng", [128, 1])
    E["bf"] = ext("bf", [128, 1])
    outT = nc.dram_tensor("outT", [D, TOK], F32, kind="ExternalOutput").ap()

    GA_in = itn("GA_in", [NCORE * TOK, D], BF16); GA_out = itn("GA_out", [NCORE * TOK, D], BF16)
    GB_in = itn("GB_in", [NCORE * TOK, D], F32); GB_out = itn("GB_out", [NCORE * TOK, D], F32)
    GF_in = itn("GF_in", [NCORE * 8, TOK], F32); GF_out = itn("GF_out", [NCORE * 8, TOK], F32)
    GH_in = itn("GH_in", [NCORE + 1, 2 * D], F32); GH_out = itn("GH_out", [NCORE + 1, 2 * D], F32)
    E3 = {}
    _zero_block(nc, E3, (GA_in, GB_in, GF_in, GH_in))

    def fm_to_heads(src, Gi, Go, dst):
        return (lambda v: (Gi.ap()[bass.ds(v["tok"], TOK), :], src.ap()[:, :]), Gi, Go,
                [lambda v: (dst.ap().rearrange("f (c t) -> c f t", c=NCORE),
                            Go.ap().rearrange("(c f) t -> c f t", c=NCORE)[:, bass.ds(v["h"], 256), :])])

    def tm_to_heads(src, Gi, Go, dst):
        return (lambda v: (Gi.ap()[bass.ds(v["tok"], TOK), :], src.ap().rearrange("h t d -> (h t) d").rearrange("(r a) d -> r (a d)", a=8)),
                Gi, Go,
                [lambda v: (dst.ap().rearrange("(c t) d -> c (t d)", c=NCORE).rearrange("c (r w) -> c r w", w=D),
                            Go.ap().rearrange("(c r) w -> c r w", c=NCORE)[:, bass.ds(v["h"], 256), :])])

    def heads_to_tok(src, Gi, Go, dst):
        return (lambda v: (Gi.ap()[bass.ds(v["tok"], TOK), :], src.ap().rearrange("f (a b) -> (f a) b", b=D)),
                Gi, Go,
                [lambda v: (dst.ap().rearrange("(c f) t -> c f t", c=NCORE),
                            Go.ap().rearrange("(c a) b -> c (a b)", c=NCORE).rearrange("c (f t) -> c f t", f=256)[:, :, bass.ds(v["tok"], TOK)])])

    qT1 = itn("qT1", [D, TOK], BF16); fT1 = itn("fT1", [D, TOK]); gs1 = itn("gs1", [D, TOK]); itok1 = itn("itok1", [8, TOK, 256], BF16)
    build_row(nc, "x", "A", TOK, tag="L1_", io=dict(hin=E["xT"], ones=E["ones"], w_mix=E["wmix_a"], qT=qT1.ap(), fT=fT1.ap(),
                                                    gs_o=gs1.ap(), itok=itok1.ap()))
    zqT = itn("zqT", [256, S], BF16); zfT = itn("zfT", [256, S]); vtok = itn("vtok", [S, 256], BF16)
    _xblock(nc, "x1", E3, [fm_to_heads(qT1, GA_in, GA_out, zqT), tm_to_heads(itok1, GA_in, GA_out, vtok),
                           fm_to_heads(fT1, GB_in, GB_out, zfT)])

    on_h = itn("on_h", [256, S], BF16)
    build_hgrn(nc, S, tag="L2_", io=dict(zqT=zqT.ap(), zfT=zfT.ap(), vtok=vtok.ap(), lbl=E["lbl"], ng=E["ng"], ident=E["ident"],
                                         ones=E["ones"], cmask=E["cmask"], rmask=E["rmask"], on=on_h.ap()))
    on3 = itn("on3", [D, TOK], BF16)
    _xblock(nc, "x2", E3, [heads_to_tok(on_h, GA_in, GA_out, on3)])

    h1 = itn("h1", [D, TOK]); qT3 = itn("qT3", [D, TOK], BF16); kT3 = itn("kT3", [D, TOK], BF16); v3 = itn("v3", [8, TOK, 256], BF16)
    build_row(nc, "A2", "B", TOK, tag="L3_", io=dict(hin=E["xT"], on=on3.ap(), gs=gs1.ap(), w_out=E["wout_a"], w1=E["w1_0"], w2=E["w2_0"],
                                                     lnp=E["lnp_0"], ones=E["ones"], w_mix=E["wmix_b"], qT=qT3.ap(), kT=kT3.ap(),
                                                     v=v3.ap(), hout=h1.ap()))
    qTs = itn("qTs", [256, S], BF16); kTs = itn("kTs", [256, S], BF16); vs = itn("vs", [S, 256], BF16)
    _xblock(nc, "x3", E3, [fm_to_heads(qT3, GA_in, GA_out, qTs), fm_to_heads(kT3, GA_in, GA_out, kTs),
                           tm_to_heads(v3, GA_in, GA_out, vs)])

    oTs = itn("oTs_h", [256, S], BF16)
    build_attn(nc, "sb", S, tag="L4_", io=dict(qT=qTs.ap(), kT=kTs.ap(), v=vs.ap(), mask=E["mask_sb"], ident=E["ident"],
                                               utri=E["utri"], nones=E["nones"], oT=oTs.ap()))
    oT5 = itn("oT5", [D, TOK], BF16)
    _xblock(nc, "x4", E3, [heads_to_tok(oTs, GA_in, GA_out, oT5)])

    h2 = itn("h2", [D, TOK]); qT5 = itn("qT5", [D, TOK], BF16); kT5 = itn("kT5", [D, TOK], BF16); v5 = itn("v5", [8, TOK, 256], BF16)
    fl5 = itn("fl5", [8, TOK])
    build_row(nc, "attn", "C", TOK, tag="L5_", io=dict(hin=h1.ap(), oT=oT5.ap(), w_out=E["wout_b"], w1=E["w1_1"], w2=E["w2_1"],
                                                       lnp=E["lnp_1"], ones=E["ones"], w_mix=E["wmix_c"], w_fl=E["w_fl"], qT=qT5.ap(), kT=kT5.ap(),
                                                       v=v5.ap(), fl=fl5.ap(), hout=h2.ap()))
    qTf = itn("qTf", [256, S], BF16); kTf = itn("kTf", [256, S], BF16); vf = itn("vf", [S, 256], BF16)
    flf = itn("flf", [S // 128, 128])
    fl_tr = (lambda v: (GF_in.ap()[bass.ds(v["p8"], 8), :], fl5.ap()[:, :]), GF_in, GF_out,
             [lambda v: (flf.ap().rearrange("(c o b) s -> c o (b s)", c=NCORE, o=1),
                         GF_out.ap().rearrange("(c h) t -> c h t", c=NCORE)[:, bass.ds(v["p"], 1), :])])
    _xblock(nc, "x5", E3, [fm_to_heads(qT5, GA_in, GA_out, qTf), fm_to_heads(kT5, GA_in, GA_out, kTf),
                           tm_to_heads(v5, GA_in, GA_out, vf), fl_tr])

    oTf = itn("oTf_h", [256, S], BF16)
    build_attn(nc, "fox", S, tag="L6_", io=dict(qT=qTf.ap(), kT=kTf.ap(), v=vf.ap(), mask=E["mask_fx"], ident=E["ident"],
                                                fl=flf.ap(), bf=E["bf"], ltri=E["ltri"], augk=E["augk"], oT=oTf.ap()))
    oT7 = itn("oT7", [D, TOK], BF16)
    _xblock(nc, "x6", E3, [heads_to_tok(oTf, GA_in, GA_out, oT7)])

    h3 = itn("h3", [D, TOK]); uT8 = itn("uT8", [D, TOK + 2]); bT7 = itn("bT7", [D, TOK])
    build_row(nc, "attn", "D", TOK, tag="L7_", io=dict(hin=h2.ap(), oT=oT7.ap(), w_out=E["wout_c"], w1=E["w1_2"], w2=E["w2_2"],
                                                       lnp=E["lnp_2"], ones=E["ones"], w_mix=E["wmix_d"], uT_o=uT8.ap()[:, 2:TOK + 2],
                                                       bT_o=bT7.ap(), hout=h3.ap()))
    halo = (lambda v: (GH_in.ap()[bass.ds(v["p1"], 1), :].rearrange("o (f t) -> (o f) t", t=2), uT8.ap()[:, TOK:TOK + 2]),
            GH_in, GH_out,
            [lambda v: (uT8.ap()[:, 0:2], GH_out.ap()[bass.ds(v["p"], 1), :].rearrange("o (f t) -> (o f) t", t=2))])
    _xblock(nc, "x7", E3, [halo])

    build_row(nc, "conv", "final", TOK, tag="L8_", io=dict(hin=h3.ap(), uT=uT8.ap(), bT=bT7.ap(), cw=E["cw"], w_out=E["wout_d"],
                                                           w1=E["w1_3"], w2=E["w2_3"], lnp=E["lnp_3"], ones=E["ones"], hout=outT))
    return nc


_cache = {}


def _tile_w(w):
    w = np.asarray(w, np.float32)
    K_, N_ = w.shape
    t = w.reshape(K_ // 2048, 16, 128, N_ // 256, 256).transpose(3, 0, 2, 1, 4)
    return np.ascontiguousarray(t).reshape(N_ // 256, K_ // 2048, 128, 4096)


def _pp(vv):
    return np.ascontiguousarray(np.asarray(vv, np.float32).reshape(16, 128).T)


def _lnp(g1, b1, g2, b2):
    return np.ascontiguousarray(np.concatenate([_pp(g1), _pp(b1), _pp(g2), _pp(b2)], axis=1))


def kernel(x, w_mix_a, norm_g_a, lb_logits, w_out_a, w_mix_b, w_out_b, w_mix_c, b_f_c, w_out_c,
           w_mix_d, conv_w_d, w_out_d, ln_mix_g, ln_mix_b, w_ff1, w_ff2, ln_ff_g, ln_ff_b):
    f32 = np.float32
    x = np.asarray(x, f32)[0]
    tt = np.arange(128)
    if "nc" not in _cache:
        nc = bass.Bass("TRN2", target_bir_lowering=False)
        build_fused(nc)
        _cache["nc"] = nc
    nc = _cache["nc"]
    shared = dict(
        wmix_a=_tile_w(w_mix_a[0]), wmix_b=_tile_w(w_mix_b[0]), wmix_c=_tile_w(np.asarray(w_mix_c[0], f32)[:, :3 * D]),
        w_fl=np.ascontiguousarray(np.asarray(w_mix_c[0], f32)[:, 3 * D:]),
        wmix_d=_tile_w(w_mix_d[0]), wout_a=_tile_w(w_out_a[0]), wout_b=_tile_w(w_out_b[0]),
        wout_c=_tile_w(w_out_c[0]), wout_d=_tile_w(w_out_d[0]),
        ones=np.ones((128, 128), f32), ident=np.eye(128, dtype=f32),
        mask_sb=np.where(tt[:, None] < tt[None, :], 0.0, NEG).astype(f32),
        mask_fx=np.where(tt[:, None] <= tt[None, :], 0.0, NEG).astype(f32),
        utri=-(tt[:, None] >= tt[None, :]).astype(f32), nones=-np.ones((128, 128), f32),
        ltri=(tt[:, None] < tt[None, :]).astype(f32),
        cmask=((tt[:, None] // 16 == tt[None, :] // 16) & (tt[:, None] <= tt[None, :])).astype(f32),
        rmask=np.stack([((tt // 16) % 4 == q).astype(f32) for q in range(4)], 1),
        ng=np.ascontiguousarray(np.asarray(norm_g_a[0], f32).reshape(128, 1)),
    )
    augk = np.zeros((128, 128), f32)
    augk[[0, 32, 64], :] = 1.0
    shared["augk"] = augk
    cw = np.asarray(conv_w_d[0], f32)
    shared["cw"] = np.ascontiguousarray(np.concatenate([_pp(cw[0]), _pp(cw[1]), _pp(cw[2])], axis=1))
    for i in range(4):
        shared[f"w1_{i}"] = _tile_w(w_ff1[i])
        shared[f"w2_{i}"] = _tile_w(w_ff2[i])
        shared[f"lnp_{i}"] = _lnp(ln_mix_g[i], ln_mix_b[i], ln_ff_g[i], ln_ff_b[i])
    lbl = np.asarray(lb_logits, f32).reshape(5, 16, 128)
    bfv = np.asarray(b_f_c[0], f32)
    in_maps = []
    for c in range(NCORE):
        m = dict(shared)
        m["xT"] = np.ascontiguousarray(x[c * TOK:(c + 1) * TOK].T)
        m["lbl"] = np.ascontiguousarray(lbl[:, 2 * c:2 * c + 2, :].transpose(2, 1, 0).reshape(128, 10))
        m["bf"] = np.full((128, 1), bfv[c], f32)
        in_maps.append(m)
    res = run_bass_kernel_spmd(nc, in_maps, core_ids=list(range(NCORE)))
    out = np.concatenate([np.ascontiguousarray(res.results[c]["outT"].T) for c in range(NCORE)], axis=0)
    return out.astype(f32)[None]
```
